# Optimizing a Trainium2 kernel written in Bass

```python
import math
import jax, jax.numpy as jnp
from jax import lax
import numpy as np

D_MODEL = 1024
BATCH = 4
SEQ = 8192
DEPTH = 1

GRID_W = 64
N_META = 16
NA_HEADS = 8
NA_HEAD_DIM = 64
NA_WIN_ROWS = 8
NA_WIN_COLS = 16
DIFF_HEADS = 4
DIFF_HEAD_DIM = 64
NA_WIDTH = NA_HEADS * NA_HEAD_DIM
DIFF_WIDTH = DIFF_HEADS * 2 * DIFF_HEAD_DIM
N_BRANCH = 2
IN_SPLITS = (NA_WIDTH, NA_WIDTH, NA_WIDTH, DIFF_WIDTH, DIFF_WIDTH, DIFF_WIDTH, D_MODEL, D_MODEL)
IN_COLS = sum(IN_SPLITS)
D_FF = -(-(8 * D_MODEL) // (3 * 256)) * 256
Q_BLOCK = 128
ROPE_THETA = 10000.0
NORM_EPS = 1e-6
SUBLN_EPS = 1e-5

kernel_name = "hybrid_natten_diffattn_gated_block"


def rmsnorm(x, g, eps=NORM_EPS):
    xf = x.astype(jnp.float32)
    y = xf * lax.rsqrt(jnp.mean(xf * xf, axis=-1, keepdims=True) + eps)
    return (y * g.astype(jnp.float32)).astype(x.dtype)


def rope(x, pos):
    d = x.shape[-1]
    half = d // 2
    inv = ROPE_THETA ** (-jnp.arange(half, dtype=jnp.float32) / half)
    ang = pos.astype(jnp.float32)[:, None] * inv[None, :]
    shp = (1, pos.shape[0]) + (1,) * (x.ndim - 3) + (half,)
    cos = jnp.cos(ang).reshape(shp)
    sin = jnp.sin(ang).reshape(shp)
    xf = x.astype(jnp.float32)
    x1, x2 = xf[..., :half], xf[..., half:]
    return jnp.concatenate([x1 * cos - x2 * sin, x2 * cos + x1 * sin], axis=-1).astype(x.dtype)


def neighbourhood_attention(q, k, v, rpb):
    B, L, H, d = q.shape
    T = L - N_META
    rows = T // GRID_W
    kr = min(NA_WIN_ROWS, rows)
    kc = NA_WIN_COLS
    scale = d ** -0.5
    qm, km, vm = q[:, :N_META], k[:, :N_META], v[:, :N_META]
    kg = k[:, N_META:].reshape(B, rows, GRID_W, H, d)
    vg = v[:, N_META:].reshape(B, rows, GRID_W, H, d)
    qg = q[:, N_META:].reshape(B, rows, GRID_W, H, d)

    s_mm = jnp.einsum('bqhd,bkhd->bhqk', qm, km).astype(jnp.float32) * scale
    p_mm = jax.nn.softmax(s_mm, axis=-1).astype(v.dtype)
    out_meta = jnp.einsum('bhqk,bkhd->bqhd', p_mm, vm)

    cols = jnp.arange(GRID_W)
    col_start = jnp.clip(cols - kc // 2, 0, GRID_W - kc)
    col_idx = col_start[:, None] + jnp.arange(kc)[None, :]
    col_bias_idx = col_idx - cols[:, None] + (NA_WIN_COLS - 1)
    bias_c = rpb.astype(jnp.float32)

    def row_fn(args):
        r, q_row = args
        rs = jnp.clip(r - kr // 2, 0, rows - kr)
        k_rows = lax.dynamic_slice_in_dim(kg, rs, kr, axis=1)
        v_rows = lax.dynamic_slice_in_dim(vg, rs, kr, axis=1)
        k_win = k_rows[:, :, col_idx]
        v_win = v_rows[:, :, col_idx]
        s_win = jnp.einsum('bwhd,biwjhd->bhwij', q_row, k_win).astype(jnp.float32) * scale
        row_bias_idx = rs + jnp.arange(kr) - r + (NA_WIN_ROWS - 1)
        bias = bias_c[:, row_bias_idx[None, :, None], col_bias_idx[:, None, :]]
        s_win = (s_win + bias[None]).reshape(B, H, GRID_W, kr * kc)
        s_meta = jnp.einsum('bwhd,bkhd->bhwk', q_row, km).astype(jnp.float32) * scale
        p = jax.nn.softmax(jnp.concatenate([s_win, s_meta], axis=-1), axis=-1).astype(v.dtype)
        p_win = p[..., :kr * kc].reshape(B, H, GRID_W, kr, kc)
        p_meta = p[..., kr * kc:]
        return (jnp.einsum('bhwij,biwjhd->bwhd', p_win, v_win)
                + jnp.einsum('bhwk,bkhd->bwhd', p_meta, vm))

    out_rows = lax.map(row_fn, (jnp.arange(rows), jnp.moveaxis(qg, 1, 0)))
    out_grid = jnp.moveaxis(out_rows, 0, 1).reshape(B, T, H, d)
    return jnp.concatenate([out_meta, out_grid], axis=1)


def diff_attention(q, k, v, lam, lambda_init, subln_g):
    B, L, H, _, d = q.shape
    T = L - N_META
    scale = d ** -0.5

    def attend(q_blk):
        s = jnp.einsum('bqhcd,bkhcd->bhcqk', q_blk, k).astype(jnp.float32) * scale
        p = jax.nn.softmax(s, axis=-1)
        a = p[:, :, 0] - lam * p[:, :, 1]
        return jnp.einsum('bhqk,bkhe->bqhe', a.astype(v.dtype), v)

    out_meta = attend(q[:, :N_META])
    q_blocks = jnp.moveaxis(q[:, N_META:].reshape(B, T // Q_BLOCK, Q_BLOCK, H, 2, d), 1, 0)
    out_real = lax.map(attend, q_blocks)
    out_real = jnp.moveaxis(out_real, 0, 1).reshape(B, T, H, 2 * d)
    o = jnp.concatenate([out_meta, out_real], axis=1)
    return rmsnorm(o, subln_g, SUBLN_EPS) * (1.0 - lambda_init)


def hybrid_layer(x, pos, layer_idx, mix_norm, w_in, na_rpb, lambda_q1, lambda_k1, lambda_q2,
                 lambda_k2, diff_subln, w_na_out, w_diff_out, w_o, ffn_norm, w_gate, w_up, w_down):
    B, L, _ = x.shape
    h = rmsnorm(x, mix_norm)
    proj = h @ w_in
    na_q, na_k, na_v, df_q, df_k, df_v, g_na, g_df = jnp.split(
        proj, list(np.cumsum(IN_SPLITS)[:-1]), axis=-1)

    na_out = neighbourhood_attention(
        na_q.reshape(B, L, NA_HEADS, NA_HEAD_DIM),
        na_k.reshape(B, L, NA_HEADS, NA_HEAD_DIM),
        na_v.reshape(B, L, NA_HEADS, NA_HEAD_DIM), na_rpb)
    o_na = na_out.reshape(B, L, NA_WIDTH) @ w_na_out

    lambda_init = 0.8 - 0.6 * math.exp(-0.3 * layer_idx)
    lam = (jnp.exp(jnp.sum(lambda_q1.astype(jnp.float32) * lambda_k1.astype(jnp.float32)))
           - jnp.exp(jnp.sum(lambda_q2.astype(jnp.float32) * lambda_k2.astype(jnp.float32)))
           + lambda_init)
    dq = rope(df_q.reshape(B, L, DIFF_HEADS, 2, DIFF_HEAD_DIM), pos)
    dk = rope(df_k.reshape(B, L, DIFF_HEADS, 2, DIFF_HEAD_DIM), pos)
    dv = df_v.reshape(B, L, DIFF_HEADS, 2 * DIFF_HEAD_DIM)
    df_out = diff_attention(dq, dk, dv, lam, lambda_init, diff_subln)
    o_df = df_out.reshape(B, L, DIFF_WIDTH) @ w_diff_out

    merged = jax.nn.sigmoid(g_na) * o_na + jax.nn.sigmoid(g_df) * o_df
    x = x + merged @ w_o

    h = rmsnorm(x, ffn_norm)
    x = x + (jax.nn.silu(h @ w_gate) * (h @ w_up)) @ w_down
    return x


def setup_inputs(seed: int = 0) -> dict:
    key = jax.random.key(seed)
    ks = jax.random.split(key, 20)
    f32 = jnp.float32

    def nrm(k, shape, scale):
        return jax.random.normal(k, shape, f32) * scale

    return {
        "x": nrm(ks[0], (BATCH, SEQ, D_MODEL), 1.0),
        "meta_tokens": nrm(ks[1], (N_META, D_MODEL), 1.0),
        "mix_norm": 1.0 + nrm(ks[2], (DEPTH, D_MODEL), 0.02),
        "w_in": nrm(ks[3], (DEPTH, D_MODEL, IN_COLS), D_MODEL ** -0.5),
        "na_rpb": nrm(ks[4], (DEPTH, NA_HEADS, 2 * NA_WIN_ROWS - 1, 2 * NA_WIN_COLS - 1), 0.1),
        "lambda_q1": nrm(ks[5], (DEPTH, DIFF_HEAD_DIM), 0.1),
        "lambda_k1": nrm(ks[6], (DEPTH, DIFF_HEAD_DIM), 0.1),
        "lambda_q2": nrm(ks[7], (DEPTH, DIFF_HEAD_DIM), 0.1),
        "lambda_k2": nrm(ks[8], (DEPTH, DIFF_HEAD_DIM), 0.1),
        "diff_subln": 1.0 + nrm(ks[9], (DEPTH, 2 * DIFF_HEAD_DIM), 0.02),
        "w_na_out": nrm(ks[10], (DEPTH, NA_WIDTH, D_MODEL), NA_WIDTH ** -0.5),
        "w_diff_out": nrm(ks[11], (DEPTH, DIFF_WIDTH, D_MODEL), DIFF_WIDTH ** -0.5),
        "w_o": nrm(ks[12], (DEPTH, D_MODEL, D_MODEL), D_MODEL ** -0.5),
        "ffn_norm": 1.0 + nrm(ks[13], (DEPTH, D_MODEL), 0.02),
        "w_gate": nrm(ks[14], (DEPTH, D_MODEL, D_FF), D_MODEL ** -0.5),
        "w_up": nrm(ks[15], (DEPTH, D_MODEL, D_FF), D_MODEL ** -0.5),
        "w_down": nrm(ks[16], (DEPTH, D_FF, D_MODEL), D_FF ** -0.5),
        "final_norm": 1.0 + nrm(ks[17], (D_MODEL,), 0.02),
    }


def reference(x, meta_tokens, mix_norm, w_in, na_rpb, lambda_q1, lambda_k1, lambda_q2, lambda_k2,
              diff_subln, w_na_out, w_diff_out, w_o, ffn_norm, w_gate, w_up, w_down, final_norm):
    B = x.shape[0]
    meta = jnp.broadcast_to(meta_tokens[None].astype(x.dtype), (B, N_META, D_MODEL))
    h = jnp.concatenate([meta, x], axis=1)
    pos = jnp.arange(h.shape[1], dtype=jnp.int32)
    for l in range(DEPTH):
        h = hybrid_layer(h, pos, l, mix_norm[l], w_in[l], na_rpb[l], lambda_q1[l], lambda_k1[l],
                         lambda_q2[l], lambda_k2[l], diff_subln[l], w_na_out[l], w_diff_out[l],
                         w_o[l], ffn_norm[l], w_gate[l], w_up[l], w_down[l])
    h = rmsnorm(h, final_norm)
    return h[:, N_META:]
```

```python
import math
from contextlib import ExitStack

import numpy as np
import concourse.bass as bass
import concourse.mybir as mybir
from concourse.bass_utils import run_bass_kernel_spmd

F32 = mybir.dt.float32
BF16 = mybir.dt.bfloat16
AF = mybir.ActivationFunctionType
ALU = mybir.AluOpType

D = 1024
KC = 8
NMETA = 16
GRID_W = 64
D_FF = 2816
FC = D_FF // 128
NEG = -30000.0
LAMBDA_INIT = 0.8 - 0.6 * math.exp(-0.3 * 0)

ENGS = ("pe", "act", "dve", "pool", "sp")
BLOCKNAME = {"pe": "tensor", "act": "scalar", "dve": "vector", "pool": "gpsimd", "sp": "sync"}
SAME_ENGINE_SYNC = True
STOP_AFTER = 99


class _Stop(Exception):
    pass


class Buf:
    __slots__ = ("name", "w", "r", "sem", "cnt", "excl")

    def __init__(self, name, excl=False):
        self.name = name
        self.excl = excl
        self.w = None
        self.r = {}
        self.sem = None
        self.cnt = 0


class Prog:
    def __init__(self, nc, stack):
        self.nc = nc
        self.stack = stack
        self.sem = {e: stack.enter_context(nc.semaphore("eng_" + e)) for e in ENGS}
        self.cnt = {e: 0 for e in ENGS}
        self.eobj = {"pe": nc.tensor, "act": nc.scalar, "dve": nc.vector, "pool": nc.gpsimd, "sp": nc.sync}
        self.seen = {e: {} for e in ENGS}
        self.dmabufs = []
        self.trace = {e: [] for e in ENGS}

    def check_deadlock(self):
        val = {}
        pos = {e: 0 for e in ENGS}
        progress = True
        while progress:
            progress = False
            for e in ENGS:
                tr = self.trace[e]
                while pos[e] < len(tr):
                    waits, key, amt = tr[pos[e]]
                    if any(val.get(k, 0) < v for k, v in waits):
                        break
                    if key is not None:
                        val[key] = val.get(key, 0) + amt
                    pos[e] += 1
                    progress = True
        stuck = {e: (pos[e], len(self.trace[e])) for e in ENGS if pos[e] < len(self.trace[e])}
        if stuck:
            msg = []
            for e, (p, n) in stuck.items():
                waits, key, amt = self.trace[e][p]
                msg.append("%s stuck at %d/%d waiting %s" % (e, p, n, [(getattr(k, "name", k), v, val.get(k, 0)) for k, v in waits]))
            raise RuntimeError("DEADLOCK: " + "; ".join(msg))

    def _semof(self, key):
        if isinstance(key, str):
            return self.sem[key]
        return key.sem

    def _deps(self, eng, reads, writes):
        need = {}

        def add(k, v):
            if need.get(k, 0) < v:
                need[k] = v

        for b in reads:
            if b.w is not None:
                add(*b.w)
            if b.excl:
                for k, v in b.r.items():
                    if k != eng:
                        add(k, v)
        for b in writes:
            if b.w is not None:
                add(*b.w)
            for k, v in b.r.items():
                add(k, v)
        out = []
        seen = self.seen[eng]
        for k, v in need.items():
            if seen.get(k, 0) >= v:
                continue
            if k == eng and (eng == "pe" or eng == "sp" or not SAME_ENGINE_SYNC):
                continue
            seen[k] = v
            out.append((k, v))
        return out

    def op(self, eng, fn, reads=(), writes=()):
        waits = self._deps(eng, reads, writes)
        self.cnt[eng] += 1
        idx = self.cnt[eng]
        self._emit(eng, waits, fn, eng, 1)
        for b in reads:
            if b.r.get(eng, 0) < idx:
                b.r[eng] = idx
        for b in writes:
            b.w = (eng, idx)
            b.r = {}

    def dma(self, eng, fn, sb, reads=(), writes=()):
        waits = self._deps(eng, reads, writes)
        if sb.sem is None:
            sb.sem = self.stack.enter_context(self.nc.semaphore("dma_" + sb.name))
            self.dmabufs.append(sb)
        sb.cnt += 16
        self._emit(eng, waits, fn, sb, 16)
        for b in reads:
            b.r[sb] = sb.cnt
        for b in writes:
            b.w = (sb, sb.cnt)
            b.r = {}

    def barrier(self):
        for eng in ENGS:
            waits = []
            seen = self.seen[eng]
            for k in ENGS:
                v = self.cnt[k]
                if k == eng or v == 0 or seen.get(k, 0) >= v:
                    continue
                seen[k] = v
                waits.append((k, v))
            for b in self.dmabufs:
                if b.cnt and seen.get(b, 0) < b.cnt:
                    seen[b] = b.cnt
                    waits.append((b, b.cnt))
            if waits:
                self._emit(eng, waits, None, None, 0)

    def _emit(self, eng, waits, fn, key, amt):
        e = self.eobj[eng]
        self.trace[eng].append((list(waits), key if fn is not None else None, amt))
        for k, v in waits:
            e.wait_ge(self._semof(k), v)
        if fn is None:
            return
        inst = fn(e)
        inst.then_inc(self._semof(key), amt)


def build(ROWS):
    T = ROWS * GRID_W
    LB = T + NMETA
    R_OWN = ROWS // 2
    NOWN = R_OWN * GRID_W
    NB = NOWN // 128
    NQB = NOWN // 512
    WT = (R_OWN + 8) // 2
    NWIN = WT * 128 + NMETA
    NKT = T // 128 + 1
    assert NOWN % 512 == 0 and NB >= 4

    nc = bass.Bass("TRN2", target_bir_lowering=False)

    def din(name, shape):
        return nc.dram_tensor(name, shape, F32, kind="ExternalInput").ap()

    xb_d = din("xb", [LB, D])
    xw_d = din("xw", [NWIN, D])
    xo_d = din("xo", [NOWN, D])
    ropeb_d = din("ropeb", [2, 128, LB])
    ropeo_d = din("ropeo", [2, 128, NOWN])
    w_in_d = din("w_in", [D, 5120])
    w_no_d = din("w_na_out", [512, D])
    w_do_d = din("w_diff_out", [512, D])
    w_o_d = din("w_o", [D, D])
    w_gate_d = din("w_gate", [D, D_FF])
    w_up_d = din("w_up", [D, D_FF])
    w_down_d = din("w_down", [D_FF, D])
    gmix_d = din("mix_norm", [1, D])
    gffn_d = din("ffn_norm", [1, D])
    gfin_d = din("final_norm", [1, D])
    subln_d = din("diff_subln", [1, 128])
    lamv_d = din("lamv", [1, 256])
    bias_d = din("na_bias", [27, 128, 1024])
    ident_d = din("ident", [128, 128])
    pswap_d = din("pswap", [128, 128])
    out_d = nc.dram_tensor("out", [NOWN, D], F32, kind="ExternalOutput").ap()
    naT_d = nc.dram_tensor("naT_s", [NB, 128, 512], BF16, kind="Internal").ap()
    dfT_d = nc.dram_tensor("dfT_s", [NB, 128, 512], BF16, kind="Internal").ap()
    x2_d = nc.dram_tensor("x2_s", [NOWN, D], F32, kind="Internal").ap()

    w_in_v = w_in_d.rearrange("(c p) n -> p c n", p=128)

    with ExitStack() as st:
        P = Prog(nc, st)

        def sbt(stack, name, shape, dt):
            return stack.enter_context(nc.sbuf_tensor("sb_" + name, shape, dt))

        ps = st.enter_context(nc.psum_tensor("ps", [128, 8, 512], F32))
        PB = [Buf("pb%d" % i, excl=True) for i in range(8)]

        def psT(b):
            return ps[:, b, :].bitcast(BF16)

        ident = sbt(st, "ident", [128, 128], BF16)
        pswap = sbt(st, "pswap", [128, 128], BF16)
        junk = sbt(st, "junk", [128, D], BF16)
        xnT = sbt(st, "xnT", [128, KC, 512], BF16)
        B_ident, B_pswap, B_junk, B_xnT = Buf("ident"), Buf("pswap"), Buf("junk"), Buf("xnT")
        mhalf = sbt(st, "mhalf", [128, 1], F32)
        B_mhalf = Buf("mhalf")
        P.op("pool", lambda e: e.memset(mhalf[:], -0.5), writes=[B_mhalf])
        P.dma("pool", lambda e: e.dma_start(out=ident[:], in_=ident_d), B_ident, writes=[B_ident])
        P.dma("pool", lambda e: e.dma_start(out=pswap[:], in_=pswap_d), B_pswap, writes=[B_pswap])

        def load_w(stack, name, view, c0, c1, nch=KC):
            w = sbt(stack, name, [128, nch, c1 - c0], BF16)
            b = Buf(name)
            half = nch // 2
            P.dma("pool", lambda e: e.dma_start(out=w[:, 0:half, :], in_=view[:, 0:half, c0:c1]), b, writes=[b])
            P.dma("pool", lambda e: e.dma_start(out=w[:, half:nch, :], in_=view[:, half:nch, c0:c1]), b, writes=[b])
            return w, b

        def load_w_pieces(stack, name, view, c0, bounds, nch=KC):
            ncols = bounds[-1]
            w = sbt(stack, name, [128, nch, ncols], BF16)
            bufs = []
            for pi in range(len(bounds) - 1):
                lo, hi = bounds[pi], bounds[pi + 1]
                bb = Buf("%s_p%d" % (name, pi))
                P.dma("pool", lambda e, lo=lo, hi=hi: e.dma_start(out=w[:, :, lo:hi], in_=view[:, :, c0 + lo:c0 + hi]), bb, writes=[bb])
                bufs.append(bb)
            return w, bufs

        def load_gain(stack, name, src, n=D):
            g = sbt(stack, name, [128, n], F32)
            b = Buf(name)
            P.dma("sp", lambda e: e.dma_start(out=g[:], in_=src.partition_broadcast(128)), b, writes=[b])
            return g, b

        class NormPipe:
            def __init__(self, stack, tag, nx, tbanks):
                self.nx = nx
                self.xs = [sbt(stack, "%s_xs%d" % (tag, i), [128, D], F32) for i in range(nx)]
                self.XS = [Buf("%s_xs%d" % (tag, i)) for i in range(nx)]
                self.xn = [sbt(stack, "%s_xn%d" % (tag, i), [128, D], BF16) for i in range(2)]
                self.XN = [Buf("%s_xn%d" % (tag, i)) for i in range(2)]
                self.nst = max(2, nx)
                self.stt = [sbt(stack, "%s_st%d" % (tag, i), [128, 4], F32) for i in range(self.nst)]
                self.ST = [Buf("%s_st%d" % (tag, i)) for i in range(self.nst)]
                self.ks = 0
                self.tb = tbanks
                self.k = 0
                self.kn = 0
                self.kt = 0

            def load(self, src, n):
                k = self.k
                self.k += 1
                xi = k % self.nx
                xs, XS = self.xs[xi], self.XS[xi]
                P.dma("sp", lambda e: e.dma_start(out=xs[0:n, :], in_=src), XS, writes=[XS])
                return {"xs": xs, "XS": XS, "n": n}

            def stats(self, h):
                i = self.ks
                self.ks += 1
                stt, ST = self.stt[i % self.nst], self.ST[i % self.nst]
                rms_rstd(h["xs"], h["XS"], h["n"], stt, ST, D, 1e-6)
                h["stt"], h["ST"] = stt, ST

            def norm(self, h, gain, G):
                if "stt" not in h:
                    self.stats(h)
                i = self.kn
                self.kn += 1
                xs, XS, n, stt, ST = h["xs"], h["XS"], h["n"], h["stt"], h["ST"]
                xn, XN = self.xn[i % 2], self.XN[i % 2]
                P.op("dve", lambda e: e.scalar_tensor_tensor(out=xn[0:n, :], in0=xs[0:n, :], scalar=stt[0:n, 2:3],
                                                             in1=gain[0:n, :], op0=ALU.mult, op1=ALU.mult),
                     reads=[XS, ST, G], writes=[XN])
                h["xn"], h["XN"] = xn, XN

            def transpose(self, h, off, XT, evac="act"):
                n, xn, XN = h["n"], h["xn"], h["XN"]
                tb = self.tb[self.kt % len(self.tb)]
                self.kt += 1
                pT = psT(tb)

                def tr(e):
                    last = None
                    for c in range(KC):
                        last = e.transpose(out=pT[:, c * 128:c * 128 + n], in_=xn[0:n, c * 128:(c + 1) * 128],
                                           identity=ident[0:n, 0:n])
                    return last
                P.op("pe", tr, reads=[XN, B_ident], writes=[PB[tb]])
                src_v = pT.rearrange("p (c t) -> p c t", c=KC)[:, :, 0:n]
                if evac == "act":
                    P.op("act", lambda e: e.copy(out=xnT[:, :, off:off + n], in_=src_v), reads=[PB[tb]], writes=[XT])
                else:
                    P.op("dve", lambda e: e.tensor_copy(out=xnT[:, :, off:off + n], in_=src_v), reads=[PB[tb]], writes=[XT])

            def prefetch(self, items):
                assert len(items) <= self.nx
                hl = [self.load(s_, n_) for (s_, n_, o_) in items]
                for h in hl:
                    self.stats(h)
                return hl

            def run_group(self, items, gain, G, evac="act", XT=None, pre=None):
                XT = B_xnT if XT is None else XT
                nt = len(items)
                hl = [None] * nt
                if pre is not None:
                    hl = list(pre)
                else:
                    for t in range(min(self.nx, nt)):
                        hl[t] = self.load(items[t][0], items[t][1])

                def norm_and_refill(t):
                    self.norm(hl[t], gain, G)
                    if t + self.nx < nt:
                        hl[t + self.nx] = self.load(items[t + self.nx][0], items[t + self.nx][1])
                norm_and_refill(0)
                for t in range(nt):
                    if t + 1 < nt:
                        norm_and_refill(t + 1)
                    self.transpose(hl[t], items[t][2], XT, evac)
                return hl

            def run(self, src, n, gain, G, off, evac="act", XT=None):
                h = self.load(src, n)
                self.norm(h, gain, G)
                self.transpose(h, off, B_xnT if XT is None else XT, evac)
                return h["xs"], h["XS"]

        def rms_rstd(x, X, n, stt, ST, nfeat, eps, use_dve_sq=False, sqjunk=None, SQJ=None):
            if use_dve_sq:
                P.op("dve", lambda e: e.scalar_tensor_tensor(out=sqjunk[0:n, :], in0=x, scalar=1.0, in1=x,
                                                             op0=ALU.mult, op1=ALU.mult, accum_out=stt[0:n, 0:1]),
                     reads=[X], writes=[SQJ, ST])
            else:
                P.op("act", lambda e: e.activation(out=junk[0:n, 0:nfeat], in_=x[0:n, :], func=AF.Square,
                                                   accum_out=stt[0:n, 0:1]),
                     reads=[X], writes=[B_junk, ST])
            P.op("pool", lambda e: e.tensor_scalar(out=stt[0:n, 1:2], in0=stt[0:n, 0:1], scalar1=1.0 / nfeat, scalar2=eps,
                                                   op0=ALU.mult, op1=ALU.add), reads=[ST], writes=[ST])
            P.op("pool", lambda e: e.tensor_tensor(out=stt[0:n, 2:3], in0=stt[0:n, 1:2], in1=mhalf[0:n, 0:1], op=ALU.pow),
                 reads=[ST, B_mhalf], writes=[ST])

        def mm_group(out_ap, lhs_fn, rhs_fn, nk, lo=0, hi=None):
            hi = nk if hi is None else hi

            def f(e):
                last = None
                for c in range(lo, hi):
                    last = e.matmul(out_ap, lhsT=lhs_fn(c), rhs=rhs_fn(c), start=(c == 0), stop=(c == nk - 1))
                return last
            return f

        def body():
            with ExitStack() as na:
                nKT = sbt(na, "nKT", [128, 4, NWIN], BF16)
                nV = sbt(na, "nV", [128, WT + 1, 8, 65], BF16)
                B_nKT = [Buf("nKT%d" % g) for g in range(WT // 4 + 2)]
                B_nV = [Buf("nV%d" % t) for t in range(WT + 1)]
                gmix, G_mix = load_gain(na, "gmixA", gmix_d)
                P.op("dve", lambda e: e.memset(nV[:, :, :, 64:65], 1.0), writes=B_nV)

                with ExitStack() as ph:
                    wk, Wk = load_w(ph, "wk_na", w_in_v, 512, 1024)
                    wv, Wv = load_w(ph, "wv_na", w_in_v, 1024, 1536)
                    npipe = NormPipe(ph, "a2", 4, [0, 1])
                    groups = []
                    g0 = 0
                    while g0 < WT * 128:
                        ng = min(512, WT * 128 - g0)
                        groups.append((g0, ng))
                        g0 += ng
                    groups.append((WT * 128, NMETA))
                    meta_gi = len(groups) - 1
                    cnt = 0
                    for gi, (g0, ng) in enumerate(groups):
                        ntl = (ng + 127) // 128
                        def a2_items(g0_, ng_):
                            return [(xw_d[g0_ + t * 128:g0_ + t * 128 + min(128, ng_ - t * 128), :], min(128, ng_ - t * 128), t * 128)
                                    for t in range((ng_ + 127) // 128)]
                        if gi == 0:
                            pre_h = npipe.prefetch(a2_items(g0, ng))
                        npipe.run_group(a2_items(g0, ng), gmix, G_mix, evac="act", pre=pre_h)
                        if gi + 1 < len(groups):
                            pre_h = npipe.prefetch(a2_items(*groups[gi + 1]))
                        for j in range(4):
                            pb = 2 + (cnt % 2)
                            cnt += 1
                            P.op("pe", mm_group(ps[:, pb, 0:ng], lambda c, j=j: wk[:, c, j * 128:(j + 1) * 128],
                                                lambda c, ng=ng: xnT[:, c, 0:ng], KC),
                                 reads=[Wk, B_xnT], writes=[PB[pb]])
                            P.op("act", lambda e, pb=pb, j=j, g0=g0, ng=ng: e.copy(out=nKT[:, j, g0:g0 + ng], in_=ps[:, pb, 0:ng]),
                                 reads=[PB[pb]], writes=[B_nKT[gi]])
                        for t in range(ntl):
                            n = min(128, ng - t * 128)
                            pb = 6 + (t % 2)
                            tile = g0 // 128 + t
                            P.op("pe", mm_group(ps[0:n, pb, :], lambda c, t=t, n=n: xnT[:, c, t * 128:t * 128 + n],
                                                lambda c: wv[:, c, :], KC),
                                 reads=[Wv, B_xnT], writes=[PB[pb]])
                            P.op("dve", lambda e, pb=pb, n=n, tile=tile: e.tensor_copy(
                                out=nV[0:n, tile, :, 0:64], in_=ps[0:n, pb, :].rearrange("p (h d) -> p h d", h=8)),
                                reads=[PB[pb]], writes=[B_nV[tile]])
                    P.barrier()
                    if STOP_AFTER <= 1:
                        return

                with ExitStack() as ph:
                    wq, Wq = load_w(ph, "wq_na", w_in_v, 0, 512)
                    bias_v = bias_d.rearrange("t p n -> p t n")
                    bint = sbt(ph, "bint", [128, 5, 1024], BF16)
                    B_bint = Buf("bint")
                    bsp = [sbt(ph, "bsp%d" % i, [128, 6, 1024], BF16) for i in range(2)]
                    B_bsp = [Buf("bsp%d" % i) for i in range(2)]
                    P.dma("pool", lambda e: e.dma_start(out=bsp[0][:, 0:6, :], in_=bias_v[:, 5:11, :]), B_bsp[0], writes=[B_bsp[0]])
                    P.dma("pool", lambda e: e.dma_start(out=bsp[1][:, 0:5, :], in_=bias_v[:, 11:16, :]), B_bsp[1], writes=[B_bsp[1]])
                    if NB > 4:
                        P.dma("pool", lambda e: e.dma_start(out=bint[:, :, :], in_=bias_v[:, 0:5, :]), B_bint, writes=[B_bint])
                    if STOP_AFTER == 1.05:
                        P.barrier()
                        return
                    npipe = NormPipe(ph, "b", 3, [0])
                    QT = [sbt(ph, "QTn%d" % i, [128, 4, 128], BF16) for i in range(2)]
                    B_QT = [Buf("QTn%d" % i) for i in range(2)]
                    PT = [[sbt(ph, "PTn%d_%d" % (i, k), [128, 1024], BF16) for k in range(7)] for i in range(2)]
                    B_PT = [[Buf("PTn%d_%d" % (i, k)) for k in range(7)] for i in range(2)]
                    rc = [sbt(ph, "rcn%d" % i, [128, 8], F32) for i in range(2)]
                    B_rc = [Buf("rcn%d" % i) for i in range(2)]
                    nao = [sbt(ph, "nao%d" % i, [128, 512], BF16) for i in range(2)]
                    B_nao = [Buf("nao%d" % i) for i in range(2)]
                    stg = [sbt(ph, "nstg%d" % i, [128, 512], BF16) for i in range(2)]
                    B_stg = [Buf("nstg%d" % i) for i in range(2)]
                    B_naT = [Buf("naT%d" % j) for j in range(NB)]
                    scnt = [0]
                    XTb = [Buf("xnTb0"), Buf("xnTb1")]
                    blk = {}

                    def pattern(j):
                        if j == 0:
                            return bsp[0], B_bsp[0], list(range(0, 6))
                        if j == 1:
                            return bsp[1], B_bsp[1], list(range(1, 6))
                        if j == NB - 2:
                            return bsp[0], B_bsp[0], list(range(j, j + 5))
                        if j == NB - 1:
                            return bsp[1], B_bsp[1], list(range(j - 1, j + 5))
                        return bint, B_bint, list(range(j, j + 5))

                    def b_front(j):
                        sl = j % 2
                        btab, Bt, tiles = pattern(j)
                        klist = [(i, kt, 128) for i, kt in enumerate(tiles)] + [(None, WT, NMETA)]
                        blk[j] = (btab, Bt, klist)
                        off = sl * 128
                        npipe.transpose(bh[j], off, XTb[sl], "dve")
                        for ch2 in range(2):
                            def qproj(e, ch2=ch2):
                                last = None
                                for ch in range(ch2 * 2, ch2 * 2 + 2):
                                    for c in range(KC):
                                        last = e.matmul(ps[:, 1, ch * 128:(ch + 1) * 128], lhsT=wq[:, c, ch * 128:(ch + 1) * 128],
                                                        rhs=xnT[:, c, off:off + 128], start=(c == 0), stop=(c == KC - 1))
                                return last
                            P.op("pe", qproj, reads=[Wq, XTb[sl]], writes=[PB[1]])
                        P.op("dve", lambda e: e.tensor_scalar(out=QT[sl][:, :, :], in0=ps[:, 1, :].rearrange("p (c t) -> p c t", c=4),
                                                              scalar1=0.125, scalar2=None, op0=ALU.mult),
                             reads=[PB[1]], writes=[B_QT[sl]])

                    def b_stile(j, slot):
                        sl = j % 2
                        btab, Bt, klist = blk[j]
                        if slot >= len(klist):
                            return
                        bi, kt, ksz = klist[slot]
                        pa = 2 + 2 * (scnt[0] % 2)
                        scnt[0] += 1

                        def smm(e):
                            last = None
                            if bi is not None:
                                for par in range(2):
                                    e.matmul(ps[:, pa + par, :], lhsT=ident[:, :], rhs=btab[:, bi, par * 512:(par + 1) * 512],
                                             start=True, stop=False, skip_group_check=True)
                            for hh in range(4):
                                for par in range(2):
                                    po = par * 64
                                    last = e.matmul(ps[0:ksz, pa + par, hh * 128:(hh + 1) * 128],
                                                    lhsT=nKT[po:po + 64, hh, kt * 128:kt * 128 + ksz],
                                                    rhs=QT[sl][po:po + 64, hh, :],
                                                    start=(bi is None), stop=True, skip_group_check=True)
                            return last
                        if kt == WT:
                            rd = [B_QT[sl], B_nKT[meta_gi]]
                        else:
                            rd = [B_QT[sl], B_nKT[kt // 4], B_ident, Bt]
                        P.op("pe", smm, reads=rd, writes=[PB[pa], PB[pa + 1]])
                        P.op("act", lambda e: e.activation(
                            out=PT[sl][slot][0:ksz, :].rearrange("p (b n) -> p b n", b=2),
                            in_=ps[0:ksz, pa:pa + 2, :], func=AF.Exp),
                            reads=[PB[pa], PB[pa + 1]], writes=[B_PT[sl][slot]])

                    def b_pv(j, half, hp):
                        sl = j % 2
                        btab, Bt, klist = blk[j]
                        nk = len(klist)

                        def pv(e):
                            last = None
                            for hh in range(hp * 2, hp * 2 + 2):
                                h = half * 4 + hh
                                col = (h % 2) * 4 + h // 2
                                for slot, (bi, kt, ksz) in enumerate(klist):
                                    last = e.matmul(ps[:, 6 + half, hh * 65:(hh + 1) * 65],
                                                    lhsT=PT[sl][slot][0:ksz, col * 128:(col + 1) * 128],
                                                    rhs=nV[0:ksz, kt, h, :], start=(slot == 0), stop=(slot == nk - 1))
                            return last
                        P.op("pe", pv, reads=[B_PT[sl][k] for k in range(nk)] + [B_nV[kt] for (_, kt, _) in klist],
                             writes=[PB[6 + half]])

                    def b_epi(j, half):
                        sl = j % 2
                        ov = ps[:, 6 + half, 0:260].rearrange("p (h d) -> p h d", h=4)
                        P.op("dve", lambda e: e.reciprocal(
                            out=rc[sl][:, half * 4:(half + 1) * 4].unsqueeze(2), in_=ov[:, :, 64:65]),
                            reads=[PB[6 + half]], writes=[B_rc[sl]])
                        P.op("dve", lambda e: e.tensor_tensor(
                            out=nao[sl][:, half * 256:(half + 1) * 256].rearrange("p (h d) -> p h d", h=4),
                            in0=ov[:, :, 0:64],
                            in1=rc[sl][:, half * 4:(half + 1) * 4].unsqueeze(2).broadcast_to([128, 4, 64]), op=ALU.mult),
                            reads=[PB[6 + half], B_rc[sl]], writes=[B_nao[sl]])

                    def b_out(j):
                        sl = j % 2
                        pT0 = psT(0)

                        def otr(e):
                            last = None
                            for c in range(4):
                                last = e.transpose(out=pT0[:, c * 128:(c + 1) * 128], in_=nao[sl][:, c * 128:(c + 1) * 128],
                                                   identity=ident[:, :])
                            return last
                        P.op("pe", otr, reads=[B_nao[sl], B_ident], writes=[PB[0]])
                        P.op("act", lambda e: e.copy(out=stg[sl][:, :], in_=pT0[:, 0:512]), reads=[PB[0]], writes=[B_stg[sl]])
                        P.dma("sp", lambda e: e.dma_start(out=naT_d[j], in_=stg[sl][:, :]), B_stg[sl],
                              reads=[B_stg[sl]], writes=[B_naT[j]])

                    bh = {}

                    def b_load(j):
                        bh[j] = npipe.load(xo_d[j * 128:(j + 1) * 128, :], 128)

                    b_load(0)
                    b_load(1)
                    npipe.norm(bh[0], gmix, G_mix)
                    b_front(0)
                    for t in range(7):
                        b_stile(0, t)
                    for j in range(NB):
                        nx = j + 1 < NB
                        if j == 1:
                            P.dma("pool", lambda e: e.dma_start(out=bsp[0][:, 0:5, :], in_=bias_v[:, 16:21, :]), B_bsp[0], writes=[B_bsp[0]])
                            P.dma("pool", lambda e: e.dma_start(out=bsp[1][:, 0:6, :], in_=bias_v[:, 21:27, :]), B_bsp[1], writes=[B_bsp[1]])
                        if j + 2 < NB:
                            b_load(j + 2)
                        if nx:
                            npipe.norm(bh[j + 1], gmix, G_mix)
                        b_pv(j, 0, 0)
                        b_pv(j, 0, 1)
                        b_epi(j, 0)
                        if nx:
                            b_front(j + 1)
                            b_stile(j + 1, 0)
                            b_stile(j + 1, 1)
                        b_pv(j, 1, 0)
                        if nx:
                            b_stile(j + 1, 2)
                        b_pv(j, 1, 1)
                        b_epi(j, 1)
                        if nx:
                            b_stile(j + 1, 3)
                            b_stile(j + 1, 4)
                        b_out(j)
                        if nx:
                            b_stile(j + 1, 5)
                            b_stile(j + 1, 6)
                    P.barrier()
                    if STOP_AFTER <= 2:
                        return

            with ExitStack() as df:
                dKT = sbt(df, "dKT", [128, 4, LB], BF16)
                dV = sbt(df, "dV", [128, NKT, 4, 129], BF16)
                B_dKT = [[Buf("dKT%d_%d" % (h, g)) for g in range(T // 512 + 1)] for h in range(4)]
                B_dV = [Buf("dV%d" % t) for t in range(NKT)]
                gmix, G_mix = load_gain(df, "gmixC", gmix_d)
                P.op("dve", lambda e: e.memset(dV[:, :, :, 128:129], 1.0), writes=B_dV)
                ropeb_v = ropeb_d.rearrange("t p n -> p t n")
                ropeo_v = ropeo_d.rearrange("t p n -> p t n")
                Kb = [sbt(df, "Kb%d" % i, [128, 512], BF16) for i in range(2)]
                B_Kb = [Buf("Kb%d" % i) for i in range(2)]
                t1 = [sbt(df, "t1_0", [128, 512], F32)] * 2
                t2 = [sbt(df, "t2_0", [128, 512], F32)] * 2
                B_t1 = [Buf("t1_0")] * 2
                B_t2 = [Buf("t2_0")] * 2
                rcnt = [0]

                def rope_a(w, W, h, ng, pk):
                    i = rcnt[0] % 2
                    rcnt[0] += 1
                    P.op("pe", mm_group(ps[:, pk, 0:ng], lambda c: w[:, c, h * 128:(h + 1) * 128],
                                        lambda c: xnT[:, c, 0:ng], KC), reads=[W, B_xnT], writes=[PB[pk]])
                    P.op("act", lambda e: e.copy(out=Kb[i][:, 0:ng], in_=ps[:, pk, 0:ng]), reads=[PB[pk]], writes=[B_Kb[i]])
                    return i

                def rope_b(i, ng, cst, CST, pk, pw, dst_fn, DST):
                    P.op("pe", lambda e: e.matmul(ps[:, pw, 0:ng], lhsT=pswap[:, :], rhs=Kb[i][:, 0:ng], start=True, stop=True),
                         reads=[B_Kb[i], B_pswap], writes=[PB[pw]])
                    P.op("dve", lambda e: e.tensor_tensor(out=t1[i][:, 0:ng], in0=ps[:, pk, 0:ng], in1=cst[:, 0, 0:ng], op=ALU.mult),
                         reads=[PB[pk], CST], writes=[B_t1[i]])
                    P.op("dve", lambda e: e.tensor_tensor(out=t2[i][:, 0:ng], in0=ps[:, pw, 0:ng], in1=cst[:, 1, 0:ng], op=ALU.mult),
                         reads=[PB[pw], CST], writes=[B_t2[i]])
                    P.op("pool", lambda e: e.tensor_tensor(out=dst_fn(), in0=t1[i][:, 0:ng], in1=t2[i][:, 0:ng], op=ALU.add),
                         reads=[B_t1[i], B_t2[i]], writes=[DST])

                with ExitStack() as ph:
                    wk, Wk = load_w(ph, "wk_d", w_in_v, 2048, 2560)
                    wv, Wv = load_w(ph, "wv_d", w_in_v, 2560, 3072)
                    npipe = NormPipe(ph, "a1", 4, [0, 1])
                    cs = [sbt(ph, "cs%d" % i, [128, 2, 512], F32) for i in range(2)]
                    B_cs = [Buf("cs%d" % i) for i in range(2)]
                    groups = [(g * 512, 512) for g in range(T // 512)] + [(T, NMETA)]
                    for gi, (g0, ng) in enumerate(groups):
                        csl = gi % 2
                        P.dma("sp", lambda e, csl=csl, g0=g0, ng=ng: e.dma_start(out=cs[csl][:, :, 0:ng], in_=ropeb_v[:, :, g0:g0 + ng]),
                              B_cs[csl], writes=[B_cs[csl]])
                        ntl = (ng + 127) // 128
                        def a1_items(g0_, ng_):
                            return [(xb_d[g0_ + t * 128:g0_ + t * 128 + min(128, ng_ - t * 128), :], min(128, ng_ - t * 128), t * 128)
                                    for t in range((ng_ + 127) // 128)]
                        if gi == 0:
                            pre_h = npipe.prefetch(a1_items(g0, ng))
                        npipe.run_group(a1_items(g0, ng), gmix, G_mix, evac="act", pre=pre_h)
                        if gi + 1 < len(groups):
                            pre_h = npipe.prefetch(a1_items(*groups[gi + 1]))
                        def vtile(t):
                            n = min(128, ng - t * 128)
                            pb = 6 + (t % 2)
                            tile = g0 // 128 + t
                            P.op("pe", mm_group(ps[0:n, pb, :], lambda c: xnT[:, c, t * 128:t * 128 + n],
                                                lambda c: wv[:, c, :], KC), reads=[Wv, B_xnT], writes=[PB[pb]])
                            P.op("act", lambda e: e.copy(
                                out=dV[0:n, tile, :, 0:128], in_=ps[0:n, pb, :].rearrange("p (h d) -> p h d", h=4)),
                                reads=[PB[pb]], writes=[B_dV[tile]])
                        for h in range(4):
                            ki = rope_a(wk, Wk, h, ng, 2 + h % 2)
                            if h < ntl:
                                vtile(h)
                            rope_b(ki, ng, cs[csl], B_cs[csl], 2 + h % 2, 4 + h % 2,
                                   lambda h=h: dKT[:, h, g0:g0 + ng], B_dKT[h][gi])
                    P.barrier()
                    if STOP_AFTER <= 3:
                        return

                with ExitStack() as ph:
                    wq, Wq = load_w(ph, "wq_d", w_in_v, 1536, 2048)
                    npipe = NormPipe(ph, "c", 4, [0])
                    csq = sbt(ph, "csq", [128, 2, 512], F32)
                    B_csq = Buf("csq")
                    lamv = sbt(ph, "lamv", [128, 256], F32)
                    lst = sbt(ph, "lst", [128, 8], F32)
                    lj = sbt(ph, "lj", [128, 64], F32)
                    sgn = sbt(ph, "sgn", [128, 128], F32)
                    B_lamv, B_lst, B_lj, B_sgn = Buf("lamv"), Buf("lst"), Buf("lj"), Buf("sgn")
                    P.dma("sp", lambda e: e.dma_start(out=lamv[:], in_=lamv_d.partition_broadcast(128)), B_lamv, writes=[B_lamv])
                    P.dma("sp", lambda e: e.dma_start(out=sgn[:], in_=subln_d.partition_broadcast(128)), B_sgn, writes=[B_sgn])
                    for i in range(2):
                        P.op("dve", lambda e, i=i: e.scalar_tensor_tensor(
                            out=lj[:, :], in0=lamv[:, i * 128:i * 128 + 64], scalar=1.0, in1=lamv[:, i * 128 + 64:i * 128 + 128],
                            op0=ALU.mult, op1=ALU.mult, accum_out=lst[:, i:i + 1]),
                            reads=[B_lamv], writes=[B_lj, B_lst])
                    P.op("act", lambda e: e.activation(out=lst[:, 2:4], in_=lst[:, 0:2], func=AF.Exp), reads=[B_lst], writes=[B_lst])
                    P.op("dve", lambda e: e.tensor_tensor(out=lst[:, 4:5], in0=lst[:, 3:4], in1=lst[:, 2:3], op=ALU.subtract),
                         reads=[B_lst], writes=[B_lst])
                    P.op("dve", lambda e: e.tensor_scalar(out=lst[:, 5:6], in0=lst[:, 4:5], scalar1=-LAMBDA_INIT, scalar2=None, op0=ALU.add),
                         reads=[B_lst], writes=[B_lst])
                    P.op("dve", lambda e: e.tensor_scalar(out=sgn[:, :], in0=sgn[:, :], scalar1=1.0 - LAMBDA_INIT, scalar2=None, op0=ALU.mult),
                         reads=[B_sgn], writes=[B_sgn])
                    dQT = sbt(ph, "dQT", [128, 4, 512], BF16)
                    ocp = sbt(ph, "ocp", [128, 3, 387], F32)
                    B_ocp = Buf("ocp")
                    B_dQT = [Buf("dQT%d" % h) for h in range(4)]
                    PT = [sbt(ph, "PTd%d" % i, [128, 1024], BF16) for i in range(3)]
                    B_PT = [Buf("PTd%d" % i) for i in range(3)]
                    est = [sbt(ph, "est%d" % i, [128, 8], F32) for i in range(2)]
                    B_est = [Buf("est%d" % i) for i in range(2)]
                    etmp = [sbt(ph, "etmp%d" % i, [128, 128], F32) for i in range(2)]
                    B_etmp = [Buf("etmp%d" % i) for i in range(2)]
                    eo = [sbt(ph, "eo%d" % i, [128, 128], F32) for i in range(2)]
                    B_eo = [Buf("eo%d" % i) for i in range(2)]
                    ej = sbt(ph, "ej", [128, 128], F32)
                    B_ej = Buf("ej")
                    dfo = [sbt(ph, "dfo%d" % i, [128, 512], BF16) for i in range(4)]
                    B_dfo = [Buf("dfo%d" % i) for i in range(4)]
                    stg = [sbt(ph, "dstg%d" % i, [128, 512], BF16) for i in range(2)]
                    B_stg = [Buf("dstg%d" % i) for i in range(2)]
                    B_dfT = [Buf("dfT%d" % j) for j in range(NB)]
                    ecnt = 0
                    pcnt = 0
                    def c_items(qb_):
                        return [(xo_d[qb_ * 512 + t * 128:qb_ * 512 + (t + 1) * 128, :], 128, t * 128) for t in range(4)]

                    def c_cs_load(qb_):
                        P.dma("sp", lambda e: e.dma_start(out=csq[:, :, :], in_=ropeo_v[:, :, qb_ * 512:(qb_ + 1) * 512]),
                              B_csq, writes=[B_csq])

                    c_cs_load(0)
                    pq = [npipe.prefetch(c_items(0))]
                    npipe.run_group(c_items(0), gmix, G_mix, evac="dve", pre=pq[0])

                    def c_qproj(q):
                        kis = {}
                        for step in range(5):
                            if step < 4:
                                kis[step] = rope_a(wq, Wq, step, 512, 1 + 2 * (step % 2))
                            if step >= 1:
                                hq = step - 1
                                rope_b(kis[hq], 512, csq, B_csq, 1 + 2 * (hq % 2), 2 + 2 * (hq % 2),
                                       lambda hq=hq: dQT[:, hq, :], B_dQT[hq])
                        if q + 1 < NQB:
                            c_cs_load(q + 1)
                            pq[0] = [npipe.load(s_, n_) for (s_, n_, o_) in c_items(q + 1)]

                    pend_out = [None]

                    def c_out(q):
                        pT0 = psT(0)
                        for qs in range(4):
                            sl = qs % 2
                            j = q * 4 + qs

                            def otr(e):
                                last = None
                                for c in range(4):
                                    last = e.transpose(out=pT0[:, c * 128:(c + 1) * 128], in_=dfo[qs][:, c * 128:(c + 1) * 128],
                                                       identity=ident[:, :])
                                return last
                            P.op("pe", otr, reads=[B_dfo[qs], B_ident], writes=[PB[0]])
                            P.op("dve", lambda e: e.tensor_copy(out=stg[sl][:, :], in_=pT0[:, 0:512]), reads=[PB[0]], writes=[B_stg[sl]])
                            P.dma("sp", lambda e: e.dma_start(out=dfT_d[j], in_=stg[sl][:, :]), B_stg[sl],
                                  reads=[B_stg[sl]], writes=[B_dfT[j]])

                    c_qproj(0)
                    for qb in range(NQB):
                        def banks(k):
                            return 1 + 2 * (k % 2)

                        def qk(h, kt, pa):
                            ksz = 128 if kt < NKT - 1 else NMETA

                            def f(e):
                                last = None
                                for c in range(2):
                                    last = e.matmul(ps[0:ksz, pa + c, :], lhsT=dKT[c * 64:(c + 1) * 64, h, kt * 128:kt * 128 + ksz],
                                                    rhs=dQT[c * 64:(c + 1) * 64, h, :], start=True, stop=True)
                                return last
                            P.op("pe", f, reads=[B_dKT[h][min(kt // 4, T // 512)], B_dQT[h]], writes=[PB[pa], PB[pa + 1]])

                        for h in range(4):
                            base = pcnt
                            if h == 0:
                                qk(0, 0, banks(base))
                                qk(0, 1, banks(base + 1))
                            for kt in range(NKT):
                                ksz = 128 if kt < NKT - 1 else NMETA
                                pa = banks(base + kt)
                                sl = (base + kt) % 3
                                P.op("act", lambda e, pa=pa, ksz=ksz, sl=sl: e.activation(
                                    out=PT[sl][0:ksz, :].rearrange("p (b n) -> p b n", b=2), in_=ps[0:ksz, pa:pa + 2, :],
                                    func=AF.Exp, scale=0.125), reads=[PB[pa], PB[pa + 1]], writes=[B_PT[sl]])
                                if h == 0 and kt == 8 and pend_out[0] is not None:
                                    c_out(pend_out[0])
                                    pend_out[0] = None
                                if h == 3 and kt == NKT // 2 and qb + 1 < NQB:
                                    npipe.run_group(c_items(qb + 1), gmix, G_mix, evac="dve", pre=pq[0])
                                if kt + 2 < NKT:
                                    qk(h, kt + 2, banks(base + kt + 2))
                                elif h < 3:
                                    qk(h + 1, kt + 2 - NKT, banks(base + kt + 2))

                                def pv(e, kt=kt, ksz=ksz, sl=sl):
                                    last = None
                                    for c in range(2):
                                        for qs in range(4):
                                            a = c * 4 + qs
                                            last = e.matmul(ps[:, 5 + a // 3, (a % 3) * 129:(a % 3) * 129 + 129],
                                                            lhsT=PT[sl][0:ksz, c * 512 + qs * 128:c * 512 + (qs + 1) * 128],
                                                            rhs=dV[0:ksz, kt, h, :], start=(kt == 0 and a % 3 == 0),
                                                            stop=(kt == NKT - 1), skip_group_check=True)
                                    return last
                                P.op("pe", pv, reads=[B_PT[sl], B_dV[kt]], writes=[PB[5], PB[6], PB[7]])
                            pcnt = base + NKT
                            P.op("dve", lambda e: e.tensor_copy(out=ocp[:, :, :], in_=ps[:, 5:8, 0:387]),
                                 reads=[PB[5], PB[6], PB[7]], writes=[B_ocp])
                            if h == 0 and qb + 1 < NQB:
                                for hnd in pq[0]:
                                    npipe.stats(hnd)
                            if h == 3 and qb + 1 < NQB:
                                c_qproj(qb + 1)
                            for qs in range(4):
                                i = ecnt % 2
                                ecnt += 1
                                a0, a1 = qs, 4 + qs
                                O0 = ocp[:, a0 // 3, (a0 % 3) * 129:(a0 % 3) * 129 + 129]
                                O1 = ocp[:, a1 // 3, (a1 % 3) * 129:(a1 % 3) * 129 + 129]
                                E, BE = est[i], B_est[i]
                                P.op("dve", lambda e, E=E, O0=O0: e.reciprocal(out=E[:, 0:1], in_=O0[:, 128:129]),
                                     reads=[B_ocp], writes=[BE])
                                P.op("dve", lambda e, E=E, O1=O1: e.reciprocal(out=E[:, 1:2], in_=O1[:, 128:129]),
                                     reads=[B_ocp], writes=[BE])
                                P.op("dve", lambda e, E=E: e.tensor_tensor(out=E[:, 2:3], in0=E[:, 1:2], in1=lst[:, 5:6], op=ALU.mult),
                                     reads=[BE, B_lst], writes=[BE])
                                P.op("dve", lambda e, E=E, O1=O1, i=i: e.tensor_scalar(out=etmp[i][:, :], in0=O1[:, 0:128], scalar1=E[:, 2:3],
                                                                                  scalar2=None, op0=ALU.mult),
                                     reads=[B_ocp, BE], writes=[B_etmp[i]])
                                P.op("dve", lambda e, E=E, O0=O0, i=i: e.scalar_tensor_tensor(
                                    out=eo[i][:, :], in0=O0[:, 0:128], scalar=E[:, 0:1], in1=etmp[i][:, :], op0=ALU.mult, op1=ALU.add),
                                    reads=[B_ocp, BE, B_etmp[i]], writes=[B_eo[i]])
                                P.op("dve", lambda e, E=E, i=i: e.scalar_tensor_tensor(
                                    out=ej[:, :], in0=eo[i][:, :], scalar=1.0, in1=eo[i][:, :], op0=ALU.mult, op1=ALU.mult,
                                    accum_out=E[:, 3:4]), reads=[B_eo[i]], writes=[B_ej, BE])
                                P.op("pool", lambda e, E=E: e.tensor_scalar(out=E[:, 4:5], in0=E[:, 3:4], scalar1=1.0 / 128, scalar2=1e-5,
                                                                            op0=ALU.mult, op1=ALU.add), reads=[BE], writes=[BE])
                                P.op("pool", lambda e, E=E: e.tensor_tensor(out=E[:, 5:6], in0=E[:, 4:5], in1=mhalf[:, 0:1], op=ALU.pow),
                                     reads=[BE, B_mhalf], writes=[BE])
                                P.op("dve", lambda e, E=E, i=i, qs=qs: e.scalar_tensor_tensor(
                                    out=dfo[qs][:, h * 128:(h + 1) * 128], in0=eo[i][:, :], scalar=E[:, 5:6], in1=sgn[:, :],
                                    op0=ALU.mult, op1=ALU.mult), reads=[B_eo[i], BE, B_sgn], writes=[B_dfo[qs]])
                        if qb + 1 < NQB:
                            pend_out[0] = qb
                        else:
                            c_out(qb)
                    P.barrier()
                    if STOP_AFTER <= 4:
                        return

            B_x2 = [Buf("x2_%d" % j) for j in range(NB)]
            ffb = [0, 2 * 128, 6 * 128, 11 * 128, 16 * 128, FC * 128]
            wg_v = w_gate_d.rearrange("(c p) n -> p c n", p=128)
            wu_v = w_up_d.rearrange("(c p) n -> p c n", p=128)
            dd = st.enter_context(ExitStack())
            wgt = sbt(dd, "wgt", [128, KC, D_FF], BF16)
            Wgt_p = [Buf("wgt_p%d" % pi) for pi in range(len(ffb) - 1)]
            with ExitStack() as ph:
                gmix, G_mix = load_gain(ph, "gmixD", gmix_d)
                wg, Wg2 = load_w_pieces(ph, "wg", w_in_v, 3072, [0, 512, 1024, 1536, 2048])
                wno, Wno = load_w(ph, "wno", w_no_d.rearrange("(c p) n -> p c n", p=128), 0, D, nch=4)
                wdo, Wdo = load_w(ph, "wdo", w_do_d.rearrange("(c p) n -> p c n", p=128), 0, D, nch=4)
                wo, Wo = load_w(ph, "wo", w_o_d.rearrange("(c p) n -> p c n", p=128), 0, D)
                for pi in range(len(ffb) - 1):
                    P.dma("pool", lambda e, lo=ffb[pi], hi=ffb[pi + 1]: e.dma_start(out=wgt[:, :, lo:hi], in_=wg_v[:, :, lo:hi]),
                          Wgt_p[pi], writes=[Wgt_p[pi]])
                npipe = NormPipe(ph, "d1", 4, [0])
                XT = [Buf("xnTa"), Buf("xnTb")]
                naTt = [sbt(ph, "naTt%d" % i, [128, 512], BF16) for i in range(3)]
                dfTt = [sbt(ph, "dfTt%d" % i, [128, 512], BF16) for i in range(3)]
                B_naTt = [Buf("naTt%d" % i) for i in range(3)]
                B_dfTt = [Buf("dfTt%d" % i) for i in range(3)]
                sgt = sbt(ph, "sgt", [128, 2048], BF16)
                B_sgn_, B_sgd_ = Buf("sgt_na"), Buf("sgt_df")
                m1 = sbt(ph, "m1", [128, D], F32)
                m2 = sbt(ph, "m2", [128, D], F32)
                mb = sbt(ph, "mb", [128, D], BF16)
                mT = sbt(ph, "mT", [128, KC, 128], BF16)
                B_m1, B_m2, B_mb, B_mT = Buf("m1"), Buf("m2"), Buf("mb"), Buf("mT")
                x2s = [sbt(ph, "x2st%d" % i, [128, D], F32) for i in range(2)]
                B_x2s = [Buf("x2st%d" % i) for i in range(2)]
                hs = {}

                def d1_loads(j):
                    hs[j] = npipe.load(xo_d[j * 128:(j + 1) * 128, :], 128)
                    sl = j % 3
                    P.dma("sp", lambda e: e.dma_start(out=naTt[sl][:, :], in_=naT_d[j]), B_naTt[sl],
                          reads=[B_naT[j]], writes=[B_naTt[sl]])
                    P.dma("sp", lambda e: e.dma_start(out=dfTt[sl][:, :], in_=dfT_d[j]), B_dfTt[sl],
                          reads=[B_dfT[j]], writes=[B_dfTt[sl]])

                def d1_gate(j, which):
                    off = (j % 2) * 128
                    b0 = 1 + 2 * which
                    for half in range(2):
                        P.op("pe", mm_group(ps[:, b0 + half, :], lambda c: xnT[:, c, off:off + 128],
                                            lambda c: wg[:, c, which * 1024 + half * 512:which * 1024 + (half + 1) * 512], KC),
                             reads=[Wg2[which * 2 + half], XT[j % 2]], writes=[PB[b0 + half]])
                    P.op("act", lambda e: e.activation(out=sgt[:, which * 1024:(which + 1) * 1024].rearrange("p (b n) -> p b n", b=2),
                                                       in_=ps[:, b0:b0 + 2, :], func=AF.Sigmoid),
                         reads=[PB[b0], PB[b0 + 1]], writes=[B_sgd_ if which else B_sgn_])

                def d1_branch(j, which):
                    sl = j % 3
                    src_t, SRC, w, W = (dfTt[sl], B_dfTt[sl], wdo, Wdo) if which else (naTt[sl], B_naTt[sl], wno, Wno)
                    for half in range(2):
                        P.op("pe", mm_group(ps[:, 5 + half, :], lambda c: src_t[:, c * 128:(c + 1) * 128],
                                            lambda c: w[:, c, half * 512:(half + 1) * 512], 4),
                             reads=[W, SRC], writes=[PB[5 + half]])
                    dst, DST = (m2, B_m2) if which else (m1, B_m1)
                    P.op("dve", lambda e: e.tensor_tensor(out=dst[:, :].rearrange("p (b n) -> p b n", b=2), in0=ps[:, 5:7, :],
                                                          in1=sgt[:, which * 1024:(which + 1) * 1024].rearrange("p (b n) -> p b n", b=2),
                                                          op=ALU.mult),
                         reads=[PB[5], PB[6], B_sgd_ if which else B_sgn_], writes=[DST])
                    if which:
                        P.op("pool", lambda e: e.tensor_tensor(out=mb[:, :], in0=m1[:, :], in1=m2[:, :], op=ALU.add),
                             reads=[B_m1, B_m2], writes=[B_mb])

                def d1_mT(j):
                    pT7 = psT(7)

                    def mtr(e):
                        last = None
                        for c in range(KC):
                            last = e.transpose(out=pT7[:, c * 128:(c + 1) * 128], in_=mb[:, c * 128:(c + 1) * 128], identity=ident[:, :])
                        return last
                    P.op("pe", mtr, reads=[B_mb, B_ident], writes=[PB[7]])
                    P.op("act", lambda e: e.copy(out=mT[:, :, :], in_=pT7.rearrange("p (c t) -> p c t", c=KC)), reads=[PB[7]], writes=[B_mT])

                def d1_x2(j):
                    sl = j % 2
                    xs, XS = hs[j]["xs"], hs[j]["XS"]
                    for half in range(2):
                        P.op("pe", mm_group(ps[:, 1 + half, :], lambda c: mT[:, c, :],
                                            lambda c: wo[:, c, half * 512:(half + 1) * 512], KC),
                             reads=[Wo, B_mT], writes=[PB[1 + half]])
                    P.op("dve", lambda e: e.tensor_tensor(out=x2s[sl][:, :].rearrange("p (b n) -> p b n", b=2), in0=ps[:, 1:3, :],
                                                          in1=xs[:, :].rearrange("p (b n) -> p b n", b=2), op=ALU.add),
                         reads=[PB[1], PB[2], XS], writes=[B_x2s[sl]])
                    P.dma("sp", lambda e: e.dma_start(out=x2_d[j * 128:(j + 1) * 128, :], in_=x2s[sl][:, :]), B_x2s[sl],
                          reads=[B_x2s[sl]], writes=[B_x2[j]])

                d1_loads(0)
                d1_loads(1)
                npipe.norm(hs[0], gmix, G_mix)
                npipe.transpose(hs[0], 0, XT[0], "act")
                for j in range(NB):
                    if j + 2 < NB:
                        d1_loads(j + 2)
                    d1_gate(j, 0)
                    if j + 1 < NB:
                        npipe.norm(hs[j + 1], gmix, G_mix)
                    if j >= 1:
                        d1_mT(j - 1)
                    d1_branch(j, 0)
                    d1_gate(j, 1)
                    if j >= 1:
                        d1_x2(j - 1)
                    if j + 1 < NB:
                        npipe.transpose(hs[j + 1], ((j + 1) % 2) * 128, XT[(j + 1) % 2], "act")
                    d1_branch(j, 1)
                d1_mT(NB - 1)
                d1_x2(NB - 1)
                P.barrier()
                if STOP_AFTER <= 5:
                    return

            with ExitStack() as ph:
                gffn, G_ffn = load_gain(ph, "gffn", gffn_d)
                gfin, G_fin = load_gain(ph, "gfin", gfin_d)
                wup = sbt(ph, "wup", [128, KC, D_FF], BF16)
                Wup_p = [Buf("wup_p%d" % pi) for pi in range(len(ffb) - 1)]

                def ffpiece(c):
                    for pi in range(len(ffb) - 1):
                        if c * 128 < ffb[pi + 1]:
                            return pi
                npipe = NormPipe(ph, "d2", 4, [0])
                XT = [Buf("xnTa2"), Buf("xnTb2")]
                G2 = 256
                NG = NOWN // G2
                actT = sbt(ph, "actT", [128, FC, G2], BF16)
                B_actT = [Buf("actT%d" % c) for c in range(FC)]
                sgl = [sbt(ph, "sgl%d" % i, [128, G2], F32) for i in range(2)]
                B_sgl = [Buf("sgl%d" % i) for i in range(2)]
                x3 = sbt(ph, "x3", [128, D], F32)
                B_x3 = Buf("x3")
                fst = sbt(ph, "fst", [128, 4], F32)
                B_fst = Buf("fst")
                ost = [sbt(ph, "ost%d" % i, [128, D], F32) for i in range(2)]
                B_ost = [Buf("ost%d" % i) for i in range(2)]
                hs = {}
                cnts = {"o": 0, "c": 0}

                def d2_loads(g):
                    for t in range(G2 // 128):
                        j = g * (G2 // 128) + t
                        hs[g, t] = npipe.load(x2_d[j * 128:(j + 1) * 128, :], 128)

                def d2_norm(g):
                    for t in range(G2 // 128):
                        npipe.norm(hs[g, t], gffn, G_ffn)

                def d2_T(g):
                    for t in range(G2 // 128):
                        npipe.transpose(hs[g, t], (g % 2) * G2 + t * 128, XT[g % 2], "act")

                def d2_normT(g):
                    d2_norm(g)
                    d2_T(g)

                def d2_gu(g, c):
                    off = (g % 2) * G2
                    ccnt = cnts["c"]
                    cnts["c"] += 1
                    pg = 1 + 2 * (ccnt % 2)
                    pu = pg + 1
                    i = ccnt % 2
                    P.op("pe", mm_group(ps[:, pg, 0:G2], lambda k: wgt[:, k, c * 128:(c + 1) * 128],
                                        lambda k: xnT[:, k, off:off + G2], KC), reads=[Wgt_p[ffpiece(c)], XT[g % 2]], writes=[PB[pg]])
                    P.op("pe", mm_group(ps[:, pu, 0:G2], lambda k: wup[:, k, c * 128:(c + 1) * 128],
                                        lambda k: xnT[:, k, off:off + G2], KC), reads=[Wup_p[ffpiece(c)], XT[g % 2]], writes=[PB[pu]])
                    P.op("act", lambda e: e.activation(out=sgl[i][:, :], in_=ps[:, pg, 0:G2], func=AF.Silu),
                         reads=[PB[pg]], writes=[B_sgl[i]])
                    P.op("dve", lambda e: e.tensor_tensor(out=actT[:, c, :], in0=ps[:, pu, 0:G2], in1=sgl[i][:, :], op=ALU.mult),
                         reads=[PB[pu], B_sgl[i]], writes=[B_actT[c]])

                def d2_out(g):
                    for t in range(G2 // 128):
                        j = g * (G2 // 128) + t
                        xs, XS = hs[g, t]["xs"], hs[g, t]["XS"]
                        ybank = (5, 6) if t % 2 == 0 else (7, 0)
                        for half in range(2):
                            yb = ybank[half]
                            for lo in (0, FC // 2):
                                P.op("pe", mm_group(ps[:, yb, :], lambda c: actT[:, c, t * 128:(t + 1) * 128],
                                                    lambda c: wdn[:, c, half * 512:(half + 1) * 512], FC, lo, lo + FC // 2),
                                     reads=[Wdn] + B_actT, writes=[PB[yb]])
                        for half in range(2):
                            yb = ybank[half]
                            P.op("dve", lambda e: e.tensor_tensor(out=x3[:, half * 512:(half + 1) * 512], in0=ps[:, yb, :],
                                                                  in1=xs[:, half * 512:(half + 1) * 512], op=ALU.add),
                                 reads=[PB[yb], XS], writes=[B_x3])
                        rms_rstd(x3, B_x3, 128, fst, B_fst, D, 1e-6)
                        oi = cnts["o"] % 2
                        cnts["o"] += 1
                        P.op("dve", lambda e: e.scalar_tensor_tensor(out=ost[oi][:, :], in0=x3[:, :], scalar=fst[:, 2:3], in1=gfin[:, :],
                                                                     op0=ALU.mult, op1=ALU.mult),
                             reads=[B_x3, B_fst, G_fin], writes=[B_ost[oi]])
                        P.dma("sp", lambda e: e.dma_start(out=out_d[j * 128:(j + 1) * 128, :], in_=ost[oi][:, :]), B_ost[oi],
                              reads=[B_ost[oi]])

                d2_loads(0)
                first_x = [hs[0, t]["XS"] for t in range(G2 // 128)] + [G_ffn]
                for pi in range(len(ffb) - 1):
                    P.dma("pool", lambda e, lo=ffb[pi], hi=ffb[pi + 1]: e.dma_start(out=wup[:, :, lo:hi], in_=wu_v[:, :, lo:hi]),
                          Wup_p[pi], reads=first_x if pi == 0 else (), writes=[Wup_p[pi]])
                wdn, Wdn = load_w(ph, "wdn", w_down_d.rearrange("(c p) n -> p c n", p=128), 0, D, nch=FC)
                d2_normT(0)
                for g in range(NG):
                    if g + 1 < NG:
                        d2_loads(g + 1)
                    for c in range(FC):
                        d2_gu(g, c)
                        if c == 3 and g + 1 < NG:
                            d2_norm(g + 1)
                        if c == FC // 2 + 2 and g + 1 < NG:
                            d2_T(g + 1)
                    d2_out(g)
                P.barrier()
                if STOP_AFTER <= 6:
                    return
        body()
        P.check_deadlock()
    return nc


def _rope_tables(pos):
    half = 32
    inv = (np.float32(10000.0) ** (-np.arange(half, dtype=np.float32) / np.float32(half))).astype(np.float32)
    ang = pos.astype(np.float32)[None, :] * inv[:, None]
    cos = np.cos(ang).astype(np.float32)
    sin = np.sin(ang).astype(np.float32)
    tab = np.empty((2, 128, pos.shape[0]), np.float32)
    for p in range(128):
        d = p % 64
        f = d % 32
        tab[0, p] = cos[f]
        tab[1, p] = -sin[f] if d < 32 else sin[f]
    return tab


def _na_bias_tables(rpb, s, ROWS):
    R_OWN = ROWS // 2
    NB = R_OWN // 2
    g0 = s * R_OWN - 4
    kr = min(8, ROWS)
    kc = np.arange(GRID_W)
    qc = np.arange(GRID_W)
    cs = np.clip(qc - 8, 0, GRID_W - 16)
    colvalid = (kc[:, None] >= cs[None, :]) & (kc[:, None] < cs[None, :] + 16)
    colidx = np.clip(kc[:, None] - qc[None, :] + 15, 0, 30)

    def tile_table(j, kt):
        tab = np.full((128, 8, 128), NEG, np.float32)
        for ko in range(2):
            krow = g0 + 2 * kt + ko
            if krow < 0 or krow >= ROWS:
                continue
            for qo in range(2):
                r = s * R_OWN + 2 * j + qo
                rs = min(max(r - kr // 2, 0), ROWS - kr)
                if not (rs <= krow < rs + kr):
                    continue
                ridx = krow - r + 7
                vals = rpb[:, ridx, :][:, colidx]
                blk = np.where(colvalid[None], vals, np.float32(NEG)).astype(np.float32)
                tab[ko * 64:(ko + 1) * 64, :, qo * 64:(qo + 1) * 64] = blk.transpose(1, 0, 2)
        return np.ascontiguousarray(tab[:, [0, 2, 4, 6, 1, 3, 5, 7], :]).reshape(128, 1024)

    jint = 2 if NB > 4 else 0
    spec = ([(jint, jint + i) for i in range(5)] + [(0, i) for i in range(6)] + [(1, 1 + i) for i in range(5)]
            + [(NB - 2, NB - 2 + i) for i in range(5)] + [(NB - 1, NB - 2 + i) for i in range(6)])
    if s == 1 or NB <= 4:
        pass
    return np.stack([tile_table(j, kt) for (j, kt) in spec], 0)


def make_in_maps(inputs, ROWS, nbatch):
    T = ROWS * GRID_W
    R_OWN = ROWS // 2
    NOWN = R_OWN * GRID_W
    WT = (R_OWN + 8) // 2
    f = lambda a: np.ascontiguousarray(np.asarray(a, dtype=np.float32))
    x = f(inputs["x"])
    meta = f(inputs["meta_tokens"])
    w_in = f(inputs["w_in"][0])
    shared = {
        "w_in": w_in,
        "w_na_out": f(inputs["w_na_out"][0]), "w_diff_out": f(inputs["w_diff_out"][0]), "w_o": f(inputs["w_o"][0]),
        "w_gate": f(inputs["w_gate"][0]), "w_up": f(inputs["w_up"][0]), "w_down": f(inputs["w_down"][0]),
        "mix_norm": f(inputs["mix_norm"][0:1]), "ffn_norm": f(inputs["ffn_norm"][0:1]),
        "final_norm": f(inputs["final_norm"]).reshape(1, D), "diff_subln": f(inputs["diff_subln"][0:1]),
        "lamv": np.concatenate([f(inputs["lambda_q1"][0]), f(inputs["lambda_k1"][0]),
                                f(inputs["lambda_q2"][0]), f(inputs["lambda_k2"][0])]).reshape(1, 256),
        "ident": np.eye(128, dtype=np.float32),
    }
    psw = np.zeros((128, 128), np.float32)
    for m in range(128):
        d = m % 64
        psw[(m - d) + ((d + 32) % 64), m] = 1.0
    shared["pswap"] = psw
    posb = np.concatenate([NMETA + np.arange(T), np.arange(NMETA)])
    ropeb = _rope_tables(posb)
    rpb = f(inputs["na_rpb"][0])
    bias_s = [_na_bias_tables(rpb, s, ROWS) for s in range(2)]
    maps = []
    for b in range(nbatch):
        xbat = np.concatenate([x[b], meta], 0)
        for s in range(2):
            own0 = s * NOWN
            xw = np.zeros((WT * 128 + NMETA, D), np.float32)
            g0 = s * R_OWN - 4
            for lr in range(R_OWN + 8):
                gr = g0 + lr
                if 0 <= gr < ROWS:
                    xw[lr * 64:(lr + 1) * 64] = x[b, gr * 64:(gr + 1) * 64]
            xw[WT * 128:] = meta
            m = dict(shared)
            m["xb"] = xbat
            m["xw"] = xw
            m["xo"] = np.ascontiguousarray(x[b, own0:own0 + NOWN])
            m["ropeb"] = ropeb
            m["ropeo"] = np.ascontiguousarray(ropeb[:, :, own0:own0 + NOWN])
            m["na_bias"] = bias_s[s]
            maps.append(m)
    return maps


_NC_CACHE = {}


def run(inputs, ROWS, nbatch):
    if ROWS not in _NC_CACHE:
        _NC_CACHE[ROWS] = build(ROWS)
    nc = _NC_CACHE[ROWS]
    maps = make_in_maps(inputs, ROWS, nbatch)
    res = run_bass_kernel_spmd(nc, maps, core_ids=list(range(len(maps))))
    T = ROWS * GRID_W
    out = np.empty((nbatch, T, D), np.float32)
    NOWN = T // 2
    for b in range(nbatch):
        for s in range(2):
            out[b, s * NOWN:(s + 1) * NOWN] = res.results[b * 2 + s]["out"]
    return out


def kernel(**inputs):
    return run(inputs, 128, 4)
```

```python
import math
from contextlib import ExitStack

import numpy as np
import concourse.bass as bass
import concourse.mybir as mybir
from concourse.bass_utils import run_bass_kernel_spmd

F32 = mybir.dt.float32
BF16 = mybir.dt.bfloat16
AF = mybir.ActivationFunctionType
ALU = mybir.AluOpType

D = 1024
KC = 8
NMETA = 16
GRID_W = 64
D_FF = 2816
FC = D_FF // 128
NEG = -30000.0
LAMBDA_INIT = 0.8 - 0.6 * math.exp(-0.3 * 0)

ENGS = ("pe", "act", "dve", "pool", "sp")
BLOCKNAME = {"pe": "tensor", "act": "scalar", "dve": "vector", "pool": "gpsimd", "sp": "sync"}
SAME_ENGINE_SYNC = True
STOP_AFTER = 99


class _Stop(Exception):
    pass


class Buf:
    __slots__ = ("name", "w", "r", "sem", "cnt", "excl")

    def __init__(self, name, excl=False):
        self.name = name
        self.excl = excl
        self.w = None
        self.r = {}
        self.sem = None
        self.cnt = 0


class Prog:
    def __init__(self, nc, stack):
        self.nc = nc
        self.stack = stack
        self.sem = {e: stack.enter_context(nc.semaphore("eng_" + e)) for e in ENGS}
        self.cnt = {e: 0 for e in ENGS}
        self.eobj = {"pe": nc.tensor, "act": nc.scalar, "dve": nc.vector, "pool": nc.gpsimd, "sp": nc.sync}
        self.seen = {e: {} for e in ENGS}
        self.dmabufs = []
        self.trace = {e: [] for e in ENGS}

    def check_deadlock(self):
        val = {}
        pos = {e: 0 for e in ENGS}
        progress = True
        while progress:
            progress = False
            for e in ENGS:
                tr = self.trace[e]
                while pos[e] < len(tr):
                    waits, key, amt = tr[pos[e]]
                    if any(val.get(k, 0) < v for k, v in waits):
                        break
                    if key is not None:
                        val[key] = val.get(key, 0) + amt
                    pos[e] += 1
                    progress = True
        stuck = {e: (pos[e], len(self.trace[e])) for e in ENGS if pos[e] < len(self.trace[e])}
        if stuck:
            msg = []
            for e, (p, n) in stuck.items():
                waits, key, amt = self.trace[e][p]
                msg.append("%s stuck at %d/%d waiting %s" % (e, p, n, [(getattr(k, "name", k), v, val.get(k, 0)) for k, v in waits]))
            raise RuntimeError("DEADLOCK: " + "; ".join(msg))

    def _semof(self, key):
        if isinstance(key, str):
            return self.sem[key]
        return key.sem

    def _deps(self, eng, reads, writes):
        need = {}

        def add(k, v):
            if need.get(k, 0) < v:
                need[k] = v

        for b in reads:
            if b.w is not None:
                add(*b.w)
            if b.excl:
                for k, v in b.r.items():
                    if k != eng:
                        add(k, v)
        for b in writes:
            if b.w is not None:
                add(*b.w)
            for k, v in b.r.items():
                add(k, v)
        out = []
        seen = self.seen[eng]
        for k, v in need.items():
            if seen.get(k, 0) >= v:
                continue
            if k == eng and (eng == "pe" or eng == "sp" or not SAME_ENGINE_SYNC):
                continue
            seen[k] = v
            out.append((k, v))
        return out

    def op(self, eng, fn, reads=(), writes=()):
        waits = self._deps(eng, reads, writes)
        self.cnt[eng] += 1
        idx = self.cnt[eng]
        self._emit(eng, waits, fn, eng, 1)
        for b in reads:
            if b.r.get(eng, 0) < idx:
                b.r[eng] = idx
        for b in writes:
            b.w = (eng, idx)
            b.r = {}

    def dma(self, eng, fn, sb, reads=(), writes=()):
        waits = self._deps(eng, reads, writes)
        if sb.sem is None:
            sb.sem = self.stack.enter_context(self.nc.semaphore("dma_" + sb.name))
            self.dmabufs.append(sb)
        sb.cnt += 16
        self._emit(eng, waits, fn, sb, 16)
        for b in reads:
            b.r[sb] = sb.cnt
        for b in writes:
            b.w = (sb, sb.cnt)
            b.r = {}

    def barrier(self):
        for eng in ENGS:
            waits = []
            seen = self.seen[eng]
            for k in ENGS:
                v = self.cnt[k]
                if k == eng or v == 0 or seen.get(k, 0) >= v:
                    continue
                seen[k] = v
                waits.append((k, v))
            for b in self.dmabufs:
                if b.cnt and seen.get(b, 0) < b.cnt:
                    seen[b] = b.cnt
                    waits.append((b, b.cnt))
            if waits:
                self._emit(eng, waits, None, None, 0)

    def _emit(self, eng, waits, fn, key, amt):
        e = self.eobj[eng]
        self.trace[eng].append((list(waits), key if fn is not None else None, amt))
        for k, v in waits:
            e.wait_ge(self._semof(k), v)
        if fn is None:
            return
        inst = fn(e)
        inst.then_inc(self._semof(key), amt)


def build(ROWS):
    T = ROWS * GRID_W
    LB = T + NMETA
    R_OWN = ROWS // 2
    NOWN = R_OWN * GRID_W
    NB = NOWN // 128
    NQB = NOWN // 512
    WT = (R_OWN + 8) // 2
    NWIN = WT * 128 + NMETA
    NKT = T // 128 + 1
    assert NOWN % 512 == 0 and NB >= 4

    nc = bass.Bass("TRN2", target_bir_lowering=False)

    def din(name, shape):
        return nc.dram_tensor(name, shape, F32, kind="ExternalInput").ap()

    xb_d = din("xb", [LB, D])
    xw_d = din("xw", [NWIN, D])
    xo_d = din("xo", [NOWN, D])
    ropeb_d = din("ropeb", [2, 128, LB])
    ropeo_d = din("ropeo", [2, 128, NOWN])
    w_in_d = din("w_in", [D, 5120])
    w_no_d = din("w_na_out", [512, D])
    w_do_d = din("w_diff_out", [512, D])
    w_o_d = din("w_o", [D, D])
    w_gate_d = din("w_gate", [D, D_FF])
    w_up_d = din("w_up", [D, D_FF])
    w_down_d = din("w_down", [D_FF, D])
    gmix_d = din("mix_norm", [1, D])
    gffn_d = din("ffn_norm", [1, D])
    gfin_d = din("final_norm", [1, D])
    subln_d = din("diff_subln", [1, 128])
    lamv_d = din("lamv", [1, 256])
    bias_d = din("na_bias", [27, 128, 1024])
    ident_d = din("ident", [128, 128])
    pswap_d = din("pswap", [128, 128])
    out_d = nc.dram_tensor("out", [NOWN, D], F32, kind="ExternalOutput").ap()
    naT_d = nc.dram_tensor("naT_s", [NB, 128, 512], BF16, kind="Internal").ap()
    dfT_d = nc.dram_tensor("dfT_s", [NB, 128, 512], BF16, kind="Internal").ap()
    x2_d = nc.dram_tensor("x2_s", [NOWN, D], F32, kind="Internal").ap()

    w_in_v = w_in_d.rearrange("(c p) n -> p c n", p=128)

    with ExitStack() as st:
        P = Prog(nc, st)

        def sbt(stack, name, shape, dt):
            return stack.enter_context(nc.sbuf_tensor("sb_" + name, shape, dt))

        ps = st.enter_context(nc.psum_tensor("ps", [128, 8, 512], F32))
        PB = [Buf("pb%d" % i, excl=True) for i in range(8)]

        def psT(b):
            return ps[:, b, :].bitcast(BF16)

        ident = sbt(st, "ident", [128, 128], BF16)
        pswap = sbt(st, "pswap", [128, 128], BF16)
        junk = sbt(st, "junk", [128, D], BF16)
        xnT = sbt(st, "xnT", [128, KC, 512], BF16)
        B_ident, B_pswap, B_junk, B_xnT = Buf("ident"), Buf("pswap"), Buf("junk"), Buf("xnT")
        mhalf = sbt(st, "mhalf", [128, 1], F32)
        B_mhalf = Buf("mhalf")
        P.op("pool", lambda e: e.memset(mhalf[:], -0.5), writes=[B_mhalf])
        P.dma("pool", lambda e: e.dma_start(out=ident[:], in_=ident_d), B_ident, writes=[B_ident])
        P.dma("pool", lambda e: e.dma_start(out=pswap[:], in_=pswap_d), B_pswap, writes=[B_pswap])

        def load_w(stack, name, view, c0, c1, nch=KC):
            w = sbt(stack, name, [128, nch, c1 - c0], BF16)
            b = Buf(name)
            half = nch // 2
            P.dma("pool", lambda e: e.dma_start(out=w[:, 0:half, :], in_=view[:, 0:half, c0:c1]), b, writes=[b])
            P.dma("pool", lambda e: e.dma_start(out=w[:, half:nch, :], in_=view[:, half:nch, c0:c1]), b, writes=[b])
            return w, b

        def load_w_pieces(stack, name, view, c0, bounds, nch=KC):
            ncols = bounds[-1]
            w = sbt(stack, name, [128, nch, ncols], BF16)
            bufs = []
            for pi in range(len(bounds) - 1):
                lo, hi = bounds[pi], bounds[pi + 1]
                bb = Buf("%s_p%d" % (name, pi))
                P.dma("pool", lambda e, lo=lo, hi=hi: e.dma_start(out=w[:, :, lo:hi], in_=view[:, :, c0 + lo:c0 + hi]), bb, writes=[bb])
                bufs.append(bb)
            return w, bufs

        def load_gain(stack, name, src, n=D):
            g = sbt(stack, name, [128, n], F32)
            b = Buf(name)
            P.dma("sp", lambda e: e.dma_start(out=g[:], in_=src.partition_broadcast(128)), b, writes=[b])
            return g, b

        class NormPipe:
            def __init__(self, stack, tag, nx, tbanks):
                self.nx = nx
                self.xs = [sbt(stack, "%s_xs%d" % (tag, i), [128, D], F32) for i in range(nx)]
                self.XS = [Buf("%s_xs%d" % (tag, i)) for i in range(nx)]
                self.xn = [sbt(stack, "%s_xn%d" % (tag, i), [128, D], BF16) for i in range(2)]
                self.XN = [Buf("%s_xn%d" % (tag, i)) for i in range(2)]
                self.nst = max(2, nx)
                self.stt = [sbt(stack, "%s_st%d" % (tag, i), [128, 4], F32) for i in range(self.nst)]
                self.ST = [Buf("%s_st%d" % (tag, i)) for i in range(self.nst)]
                self.ks = 0
                self.tb = tbanks
                self.k = 0
                self.kn = 0
                self.kt = 0

            def load(self, src, n):
                k = self.k
                self.k += 1
                xi = k % self.nx
                xs, XS = self.xs[xi], self.XS[xi]
                P.dma("sp", lambda e: e.dma_start(out=xs[0:n, :], in_=src), XS, writes=[XS])
                return {"xs": xs, "XS": XS, "n": n}

            def stats(self, h):
                i = self.ks
                self.ks += 1
                stt, ST = self.stt[i % self.nst], self.ST[i % self.nst]
                rms_rstd(h["xs"], h["XS"], h["n"], stt, ST, D, 1e-6)
                h["stt"], h["ST"] = stt, ST

            def norm(self, h, gain, G):
                if "stt" not in h:
                    self.stats(h)
                i = self.kn
                self.kn += 1
                xs, XS, n, stt, ST = h["xs"], h["XS"], h["n"], h["stt"], h["ST"]
                xn, XN = self.xn[i % 2], self.XN[i % 2]
                P.op("dve", lambda e: e.scalar_tensor_tensor(out=xn[0:n, :], in0=xs[0:n, :], scalar=stt[0:n, 2:3],
                                                             in1=gain[0:n, :], op0=ALU.mult, op1=ALU.mult),
                     reads=[XS, ST, G], writes=[XN])
                h["xn"], h["XN"] = xn, XN

            def transpose(self, h, off, XT, evac="act"):
                n, xn, XN = h["n"], h["xn"], h["XN"]
                tb = self.tb[self.kt % len(self.tb)]
                self.kt += 1
                pT = psT(tb)

                def tr(e):
                    last = None
                    for c in range(KC):
                        last = e.transpose(out=pT[:, c * 128:c * 128 + n], in_=xn[0:n, c * 128:(c + 1) * 128],
                                           identity=ident[0:n, 0:n])
                    return last
                P.op("pe", tr, reads=[XN, B_ident], writes=[PB[tb]])
                src_v = pT.rearrange("p (c t) -> p c t", c=KC)[:, :, 0:n]
                if evac == "act":
                    P.op("act", lambda e: e.copy(out=xnT[:, :, off:off + n], in_=src_v), reads=[PB[tb]], writes=[XT])
                else:
                    P.op("dve", lambda e: e.tensor_copy(out=xnT[:, :, off:off + n], in_=src_v), reads=[PB[tb]], writes=[XT])

            def prefetch(self, items):
                assert len(items) <= self.nx
                hl = [self.load(s_, n_) for (s_, n_, o_) in items]
                for h in hl:
                    self.stats(h)
                return hl

            def run_group(self, items, gain, G, evac="act", XT=None, pre=None):
                XT = B_xnT if XT is None else XT
                nt = len(items)
                hl = [None] * nt
                if pre is not None:
                    hl = list(pre)
                else:
                    for t in range(min(self.nx, nt)):
                        hl[t] = self.load(items[t][0], items[t][1])

                def norm_and_refill(t):
                    if "xn" not in hl[t]:
                        self.norm(hl[t], gain, G)
                    if t + self.nx < nt:
                        hl[t + self.nx] = self.load(items[t + self.nx][0], items[t + self.nx][1])
                norm_and_refill(0)
                for t in range(nt):
                    if t + 1 < nt:
                        norm_and_refill(t + 1)
                    self.transpose(hl[t], items[t][2], XT, evac)
                return hl

            def run(self, src, n, gain, G, off, evac="act", XT=None):
                h = self.load(src, n)
                self.norm(h, gain, G)
                self.transpose(h, off, B_xnT if XT is None else XT, evac)
                return h["xs"], h["XS"]

        def rms_rstd(x, X, n, stt, ST, nfeat, eps, use_dve_sq=False, sqjunk=None, SQJ=None):
            if use_dve_sq:
                P.op("dve", lambda e: e.scalar_tensor_tensor(out=sqjunk[0:n, :], in0=x, scalar=1.0, in1=x,
                                                             op0=ALU.mult, op1=ALU.mult, accum_out=stt[0:n, 0:1]),
                     reads=[X], writes=[SQJ, ST])
            else:
                P.op("act", lambda e: e.activation(out=junk[0:n, 0:nfeat], in_=x[0:n, :], func=AF.Square,
                                                   accum_out=stt[0:n, 0:1]),
                     reads=[X], writes=[B_junk, ST])
            P.op("pool", lambda e: e.tensor_scalar(out=stt[0:n, 1:2], in0=stt[0:n, 0:1], scalar1=1.0 / nfeat, scalar2=eps,
                                                   op0=ALU.mult, op1=ALU.add), reads=[ST], writes=[ST])
            P.op("pool", lambda e: e.tensor_tensor(out=stt[0:n, 2:3], in0=stt[0:n, 1:2], in1=mhalf[0:n, 0:1], op=ALU.pow),
                 reads=[ST, B_mhalf], writes=[ST])

        def mm_group(out_ap, lhs_fn, rhs_fn, nk, lo=0, hi=None):
            hi = nk if hi is None else hi

            def f(e):
                last = None
                for c in range(lo, hi):
                    last = e.matmul(out_ap, lhsT=lhs_fn(c), rhs=rhs_fn(c), start=(c == 0), stop=(c == nk - 1))
                return last
            return f

        def body():
            with ExitStack() as na:
                nKT = sbt(na, "nKT", [128, 4, NWIN], BF16)
                nV = sbt(na, "nV", [128, WT + 1, 8, 65], BF16)
                B_nKT = [Buf("nKT%d" % g) for g in range(WT // 4 + 2)]
                B_nV = [Buf("nV%d" % t) for t in range(WT + 1)]
                gmix, G_mix = load_gain(na, "gmixA", gmix_d)
                P.op("dve", lambda e: e.memset(nV[:, :, :, 64:65], 1.0), writes=B_nV)

                with ExitStack() as ph:
                    wk, Wk = load_w(ph, "wk_na", w_in_v, 512, 1024)
                    wv, Wv = load_w(ph, "wv_na", w_in_v, 1024, 1536)
                    npipe = NormPipe(ph, "a2", 4, [0, 1])
                    groups = []
                    g0 = 0
                    while g0 < WT * 128:
                        ng = min(512, WT * 128 - g0)
                        groups.append((g0, ng))
                        g0 += ng
                    groups.append((WT * 128, NMETA))
                    meta_gi = len(groups) - 1
                    cnt = 0
                    for gi, (g0, ng) in enumerate(groups):
                        ntl = (ng + 127) // 128
                        def a2_items(g0_, ng_):
                            return [(xw_d[g0_ + t * 128:g0_ + t * 128 + min(128, ng_ - t * 128), :], min(128, ng_ - t * 128), t * 128)
                                    for t in range((ng_ + 127) // 128)]
                        if gi == 0:
                            pre_h = npipe.prefetch(a2_items(g0, ng))
                        npipe.run_group(a2_items(g0, ng), gmix, G_mix, evac="act", pre=pre_h)
                        if gi + 1 < len(groups):
                            pre_h = npipe.prefetch(a2_items(*groups[gi + 1]))
                            for hh_ in pre_h[:2]:
                                npipe.norm(hh_, gmix, G_mix)
                        for j in range(4):
                            pb = 2 + (cnt % 2)
                            cnt += 1
                            P.op("pe", mm_group(ps[:, pb, 0:ng], lambda c, j=j: wk[:, c, j * 128:(j + 1) * 128],
                                                lambda c, ng=ng: xnT[:, c, 0:ng], KC),
                                 reads=[Wk, B_xnT], writes=[PB[pb]])
                            P.op("act", lambda e, pb=pb, j=j, g0=g0, ng=ng: e.copy(out=nKT[:, j, g0:g0 + ng], in_=ps[:, pb, 0:ng]),
                                 reads=[PB[pb]], writes=[B_nKT[gi]])
                        for t in range(ntl):
                            n = min(128, ng - t * 128)
                            pb = 6 + (t % 2)
                            tile = g0 // 128 + t
                            P.op("pe", mm_group(ps[0:n, pb, :], lambda c, t=t, n=n: xnT[:, c, t * 128:t * 128 + n],
                                                lambda c: wv[:, c, :], KC),
                                 reads=[Wv, B_xnT], writes=[PB[pb]])
                            P.op("dve", lambda e, pb=pb, n=n, tile=tile: e.tensor_copy(
                                out=nV[0:n, tile, :, 0:64], in_=ps[0:n, pb, :].rearrange("p (h d) -> p h d", h=8)),
                                reads=[PB[pb]], writes=[B_nV[tile]])
                    P.barrier()
                    if STOP_AFTER <= 1:
                        return

                with ExitStack() as ph:
                    wq, Wq = load_w(ph, "wq_na", w_in_v, 0, 512)
                    bias_v = bias_d.rearrange("t p n -> p t n")
                    bint = sbt(ph, "bint", [128, 5, 1024], BF16)
                    B_bint = Buf("bint")
                    bsp = [sbt(ph, "bsp%d" % i, [128, 6, 1024], BF16) for i in range(2)]
                    B_bsp = [Buf("bsp%d" % i) for i in range(2)]
                    P.dma("pool", lambda e: e.dma_start(out=bsp[0][:, 0:6, :], in_=bias_v[:, 5:11, :]), B_bsp[0], writes=[B_bsp[0]])
                    P.dma("pool", lambda e: e.dma_start(out=bsp[1][:, 0:5, :], in_=bias_v[:, 11:16, :]), B_bsp[1], writes=[B_bsp[1]])
                    if NB > 4:
                        P.dma("pool", lambda e: e.dma_start(out=bint[:, :, :], in_=bias_v[:, 0:5, :]), B_bint, writes=[B_bint])
                    if STOP_AFTER == 1.05:
                        P.barrier()
                        return
                    npipe = NormPipe(ph, "b", 3, [0])
                    QT = [sbt(ph, "QTn%d" % i, [128, 4, 128], BF16) for i in range(2)]
                    B_QT = [Buf("QTn%d" % i) for i in range(2)]
                    PT = [[sbt(ph, "PTn%d_%d" % (i, k), [128, 1024], BF16) for k in range(7)] for i in range(2)]
                    B_PT = [[Buf("PTn%d_%d" % (i, k)) for k in range(7)] for i in range(2)]
                    rc = [sbt(ph, "rcn%d" % i, [128, 8], F32) for i in range(2)]
                    B_rc = [Buf("rcn%d" % i) for i in range(2)]
                    nao = [sbt(ph, "nao%d" % i, [128, 512], BF16) for i in range(2)]
                    B_nao = [Buf("nao%d" % i) for i in range(2)]
                    stg = [sbt(ph, "nstg%d" % i, [128, 512], BF16) for i in range(2)]
                    B_stg = [Buf("nstg%d" % i) for i in range(2)]
                    B_naT = [Buf("naT%d" % j) for j in range(NB)]
                    scnt = [0]
                    XTb = [Buf("xnTb0"), Buf("xnTb1")]
                    blk = {}

                    def pattern(j):
                        if j == 0:
                            return bsp[0], B_bsp[0], list(range(0, 6))
                        if j == 1:
                            return bsp[1], B_bsp[1], list(range(1, 6))
                        if j == NB - 2:
                            return bsp[0], B_bsp[0], list(range(j, j + 5))
                        if j == NB - 1:
                            return bsp[1], B_bsp[1], list(range(j - 1, j + 5))
                        return bint, B_bint, list(range(j, j + 5))

                    def b_front(j):
                        sl = j % 2
                        btab, Bt, tiles = pattern(j)
                        klist = [(i, kt, 128) for i, kt in enumerate(tiles)] + [(None, WT, NMETA)]
                        blk[j] = (btab, Bt, klist)
                        off = sl * 128
                        npipe.transpose(bh[j], off, XTb[sl], "dve")
                        for ch2 in range(2):
                            def qproj(e, ch2=ch2):
                                last = None
                                for ch in range(ch2 * 2, ch2 * 2 + 2):
                                    for c in range(KC):
                                        last = e.matmul(ps[:, 1, ch * 128:(ch + 1) * 128], lhsT=wq[:, c, ch * 128:(ch + 1) * 128],
                                                        rhs=xnT[:, c, off:off + 128], start=(c == 0), stop=(c == KC - 1))
                                return last
                            P.op("pe", qproj, reads=[Wq, XTb[sl]], writes=[PB[1]])
                        P.op("dve", lambda e: e.tensor_scalar(out=QT[sl][:, :, :], in0=ps[:, 1, :].rearrange("p (c t) -> p c t", c=4),
                                                              scalar1=0.125, scalar2=None, op0=ALU.mult),
                             reads=[PB[1]], writes=[B_QT[sl]])

                    def b_stile(j, slot):
                        sl = j % 2
                        btab, Bt, klist = blk[j]
                        if slot >= len(klist):
                            return
                        bi, kt, ksz = klist[slot]
                        pa = 2 + 2 * (scnt[0] % 2)
                        scnt[0] += 1

                        def smm(e):
                            last = None
                            if bi is not None:
                                for par in range(2):
                                    e.matmul(ps[:, pa + par, :], lhsT=ident[:, :], rhs=btab[:, bi, par * 512:(par + 1) * 512],
                                             start=True, stop=False, skip_group_check=True)
                            for hh in range(4):
                                for par in range(2):
                                    po = par * 64
                                    last = e.matmul(ps[0:ksz, pa + par, hh * 128:(hh + 1) * 128],
                                                    lhsT=nKT[po:po + 64, hh, kt * 128:kt * 128 + ksz],
                                                    rhs=QT[sl][po:po + 64, hh, :],
                                                    start=(bi is None), stop=True, skip_group_check=True)
                            return last
                        if kt == WT:
                            rd = [B_QT[sl], B_nKT[meta_gi]]
                        else:
                            rd = [B_QT[sl], B_nKT[kt // 4], B_ident, Bt]
                        P.op("pe", smm, reads=rd, writes=[PB[pa], PB[pa + 1]])
                        P.op("act", lambda e: e.activation(
                            out=PT[sl][slot][0:ksz, :].rearrange("p (b n) -> p b n", b=2),
                            in_=ps[0:ksz, pa:pa + 2, :], func=AF.Exp),
                            reads=[PB[pa], PB[pa + 1]], writes=[B_PT[sl][slot]])

                    def b_pv(j, half, hp):
                        sl = j % 2
                        btab, Bt, klist = blk[j]
                        nk = len(klist)

                        def pv(e):
                            last = None
                            for hh in range(hp * 2, hp * 2 + 2):
                                h = half * 4 + hh
                                col = (h % 2) * 4 + h // 2
                                for slot, (bi, kt, ksz) in enumerate(klist):
                                    last = e.matmul(ps[:, 6 + half, hh * 65:(hh + 1) * 65],
                                                    lhsT=PT[sl][slot][0:ksz, col * 128:(col + 1) * 128],
                                                    rhs=nV[0:ksz, kt, h, :], start=(slot == 0), stop=(slot == nk - 1))
                            return last
                        P.op("pe", pv, reads=[B_PT[sl][k] for k in range(nk)] + [B_nV[kt] for (_, kt, _) in klist],
                             writes=[PB[6 + half]])

                    def b_epi(j, half):
                        sl = j % 2
                        ov = ps[:, 6 + half, 0:260].rearrange("p (h d) -> p h d", h=4)
                        P.op("dve", lambda e: e.reciprocal(
                            out=rc[sl][:, half * 4:(half + 1) * 4].unsqueeze(2), in_=ov[:, :, 64:65]),
                            reads=[PB[6 + half]], writes=[B_rc[sl]])
                        P.op("dve", lambda e: e.tensor_tensor(
                            out=nao[sl][:, half * 256:(half + 1) * 256].rearrange("p (h d) -> p h d", h=4),
                            in0=ov[:, :, 0:64],
                            in1=rc[sl][:, half * 4:(half + 1) * 4].unsqueeze(2).broadcast_to([128, 4, 64]), op=ALU.mult),
                            reads=[PB[6 + half], B_rc[sl]], writes=[B_nao[sl]])

                    def b_out(j):
                        sl = j % 2
                        pT0 = psT(0)

                        def otr(e):
                            last = None
                            for c in range(4):
                                last = e.transpose(out=pT0[:, c * 128:(c + 1) * 128], in_=nao[sl][:, c * 128:(c + 1) * 128],
                                                   identity=ident[:, :])
                            return last
                        P.op("pe", otr, reads=[B_nao[sl], B_ident], writes=[PB[0]])
                        P.op("act", lambda e: e.copy(out=stg[sl][:, :], in_=pT0[:, 0:512]), reads=[PB[0]], writes=[B_stg[sl]])
                        P.dma("sp", lambda e: e.dma_start(out=naT_d[j], in_=stg[sl][:, :]), B_stg[sl],
                              reads=[B_stg[sl]], writes=[B_naT[j]])

                    bh = {}

                    def b_load(j):
                        bh[j] = npipe.load(xo_d[j * 128:(j + 1) * 128, :], 128)

                    b_load(0)
                    b_load(1)
                    npipe.norm(bh[0], gmix, G_mix)
                    b_front(0)
                    for t in range(7):
                        b_stile(0, t)
                    for j in range(NB):
                        nx = j + 1 < NB
                        if j == 1:
                            P.dma("pool", lambda e: e.dma_start(out=bsp[0][:, 0:5, :], in_=bias_v[:, 16:21, :]), B_bsp[0], writes=[B_bsp[0]])
                            P.dma("pool", lambda e: e.dma_start(out=bsp[1][:, 0:6, :], in_=bias_v[:, 21:27, :]), B_bsp[1], writes=[B_bsp[1]])
                        if j + 2 < NB:
                            b_load(j + 2)
                        if nx:
                            npipe.norm(bh[j + 1], gmix, G_mix)
                        b_pv(j, 0, 0)
                        b_pv(j, 0, 1)
                        b_epi(j, 0)
                        if nx:
                            b_front(j + 1)
                            b_stile(j + 1, 0)
                            b_stile(j + 1, 1)
                        b_pv(j, 1, 0)
                        if nx:
                            b_stile(j + 1, 2)
                        b_pv(j, 1, 1)
                        b_epi(j, 1)
                        if nx:
                            b_stile(j + 1, 3)
                            b_stile(j + 1, 4)
                        b_out(j)
                        if nx:
                            b_stile(j + 1, 5)
                            b_stile(j + 1, 6)
                    P.barrier()
                    if STOP_AFTER <= 2:
                        return

            with ExitStack() as df:
                dKT = sbt(df, "dKT", [128, 4, LB], BF16)
                dV = sbt(df, "dV", [128, NKT, 4, 129], BF16)
                B_dKT = [[Buf("dKT%d_%d" % (h, g)) for g in range(T // 512 + 1)] for h in range(4)]
                B_dV = [Buf("dV%d" % t) for t in range(NKT)]
                gmix, G_mix = load_gain(df, "gmixC", gmix_d)
                P.op("dve", lambda e: e.memset(dV[:, :, :, 128:129], 1.0), writes=B_dV)
                ropeb_v = ropeb_d.rearrange("t p n -> p t n")
                ropeo_v = ropeo_d.rearrange("t p n -> p t n")
                Kb = [sbt(df, "Kb%d" % i, [128, 512], BF16) for i in range(2)]
                B_Kb = [Buf("Kb%d" % i) for i in range(2)]
                t1 = [sbt(df, "t1_0", [128, 512], F32)] * 2
                t2 = [sbt(df, "t2_0", [128, 512], F32)] * 2
                B_t1 = [Buf("t1_0")] * 2
                B_t2 = [Buf("t2_0")] * 2
                rcnt = [0]

                def rope_a(w, W, h, ng, pk):
                    i = rcnt[0] % 2
                    rcnt[0] += 1
                    P.op("pe", mm_group(ps[:, pk, 0:ng], lambda c: w[:, c, h * 128:(h + 1) * 128],
                                        lambda c: xnT[:, c, 0:ng], KC), reads=[W, B_xnT], writes=[PB[pk]])
                    P.op("act", lambda e: e.copy(out=Kb[i][:, 0:ng], in_=ps[:, pk, 0:ng]), reads=[PB[pk]], writes=[B_Kb[i]])
                    return i

                def rope_b(i, ng, cst, CST, pk, pw, dst_fn, DST):
                    P.op("pe", lambda e: e.matmul(ps[:, pw, 0:ng], lhsT=pswap[:, :], rhs=Kb[i][:, 0:ng], start=True, stop=True),
                         reads=[B_Kb[i], B_pswap], writes=[PB[pw]])
                    P.op("dve", lambda e: e.tensor_tensor(out=t1[i][:, 0:ng], in0=ps[:, pk, 0:ng], in1=cst[:, 0, 0:ng], op=ALU.mult),
                         reads=[PB[pk], CST], writes=[B_t1[i]])
                    P.op("dve", lambda e: e.tensor_tensor(out=t2[i][:, 0:ng], in0=ps[:, pw, 0:ng], in1=cst[:, 1, 0:ng], op=ALU.mult),
                         reads=[PB[pw], CST], writes=[B_t2[i]])
                    P.op("pool", lambda e: e.tensor_tensor(out=dst_fn(), in0=t1[i][:, 0:ng], in1=t2[i][:, 0:ng], op=ALU.add),
                         reads=[B_t1[i], B_t2[i]], writes=[DST])

                with ExitStack() as ph:
                    wk, Wk = load_w(ph, "wk_d", w_in_v, 2048, 2560)
                    wv, Wv = load_w(ph, "wv_d", w_in_v, 2560, 3072)
                    npipe = NormPipe(ph, "a1", 4, [0, 1])
                    cs = [sbt(ph, "cs%d" % i, [128, 2, 512], F32) for i in range(2)]
                    B_cs = [Buf("cs%d" % i) for i in range(2)]
                    groups = [(g * 512, 512) for g in range(T // 512)] + [(T, NMETA)]
                    for gi, (g0, ng) in enumerate(groups):
                        csl = gi % 2
                        P.dma("sp", lambda e, csl=csl, g0=g0, ng=ng: e.dma_start(out=cs[csl][:, :, 0:ng], in_=ropeb_v[:, :, g0:g0 + ng]),
                              B_cs[csl], writes=[B_cs[csl]])
                        ntl = (ng + 127) // 128
                        def a1_items(g0_, ng_):
                            return [(xb_d[g0_ + t * 128:g0_ + t * 128 + min(128, ng_ - t * 128), :], min(128, ng_ - t * 128), t * 128)
                                    for t in range((ng_ + 127) // 128)]
                        if gi == 0:
                            pre_h = npipe.prefetch(a1_items(g0, ng))
                        npipe.run_group(a1_items(g0, ng), gmix, G_mix, evac="act", pre=pre_h)
                        if gi + 1 < len(groups):
                            pre_h = npipe.prefetch(a1_items(*groups[gi + 1]))
                            for hh_ in pre_h[:2]:
                                npipe.norm(hh_, gmix, G_mix)
                        def vtile(t):
                            n = min(128, ng - t * 128)
                            pb = 6 + (t % 2)
                            tile = g0 // 128 + t
                            P.op("pe", mm_group(ps[0:n, pb, :], lambda c: xnT[:, c, t * 128:t * 128 + n],
                                                lambda c: wv[:, c, :], KC), reads=[Wv, B_xnT], writes=[PB[pb]])
                            P.op("act", lambda e: e.copy(
                                out=dV[0:n, tile, :, 0:128], in_=ps[0:n, pb, :].rearrange("p (h d) -> p h d", h=4)),
                                reads=[PB[pb]], writes=[B_dV[tile]])
                        for h in range(4):
                            ki = rope_a(wk, Wk, h, ng, 2 + h % 2)
                            if h < ntl:
                                vtile(h)
                            rope_b(ki, ng, cs[csl], B_cs[csl], 2 + h % 2, 4 + h % 2,
                                   lambda h=h: dKT[:, h, g0:g0 + ng], B_dKT[h][gi])
                    P.barrier()
                    if STOP_AFTER <= 3:
                        return

                with ExitStack() as ph:
                    wq, Wq = load_w(ph, "wq_d", w_in_v, 1536, 2048)
                    npipe = NormPipe(ph, "c", 4, [0])
                    csq = sbt(ph, "csq", [128, 2, 512], F32)
                    B_csq = Buf("csq")
                    lamv = sbt(ph, "lamv", [128, 256], F32)
                    lst = sbt(ph, "lst", [128, 8], F32)
                    lj = sbt(ph, "lj", [128, 64], F32)
                    sgn = sbt(ph, "sgn", [128, 128], F32)
                    B_lamv, B_lst, B_lj, B_sgn = Buf("lamv"), Buf("lst"), Buf("lj"), Buf("sgn")
                    P.dma("sp", lambda e: e.dma_start(out=lamv[:], in_=lamv_d.partition_broadcast(128)), B_lamv, writes=[B_lamv])
                    P.dma("sp", lambda e: e.dma_start(out=sgn[:], in_=subln_d.partition_broadcast(128)), B_sgn, writes=[B_sgn])
                    for i in range(2):
                        P.op("dve", lambda e, i=i: e.scalar_tensor_tensor(
                            out=lj[:, :], in0=lamv[:, i * 128:i * 128 + 64], scalar=1.0, in1=lamv[:, i * 128 + 64:i * 128 + 128],
                            op0=ALU.mult, op1=ALU.mult, accum_out=lst[:, i:i + 1]),
                            reads=[B_lamv], writes=[B_lj, B_lst])
                    P.op("act", lambda e: e.activation(out=lst[:, 2:4], in_=lst[:, 0:2], func=AF.Exp), reads=[B_lst], writes=[B_lst])
                    P.op("dve", lambda e: e.tensor_tensor(out=lst[:, 4:5], in0=lst[:, 3:4], in1=lst[:, 2:3], op=ALU.subtract),
                         reads=[B_lst], writes=[B_lst])
                    P.op("dve", lambda e: e.tensor_scalar(out=lst[:, 5:6], in0=lst[:, 4:5], scalar1=-LAMBDA_INIT, scalar2=None, op0=ALU.add),
                         reads=[B_lst], writes=[B_lst])
                    P.op("dve", lambda e: e.tensor_scalar(out=sgn[:, :], in0=sgn[:, :], scalar1=1.0 - LAMBDA_INIT, scalar2=None, op0=ALU.mult),
                         reads=[B_sgn], writes=[B_sgn])
                    dQT = sbt(ph, "dQT", [128, 4, 512], BF16)
                    ocp = sbt(ph, "ocp", [128, 3, 387], F32)
                    B_ocp = Buf("ocp")
                    B_dQT = [Buf("dQT%d" % h) for h in range(4)]
                    PT = [sbt(ph, "PTd%d" % i, [128, 1024], BF16) for i in range(3)]
                    B_PT = [Buf("PTd%d" % i) for i in range(3)]
                    est = [sbt(ph, "est%d" % i, [128, 8], F32) for i in range(2)]
                    B_est = [Buf("est%d" % i) for i in range(2)]
                    etmp = [sbt(ph, "etmp%d" % i, [128, 128], F32) for i in range(2)]
                    B_etmp = [Buf("etmp%d" % i) for i in range(2)]
                    eo = [sbt(ph, "eo%d" % i, [128, 128], F32) for i in range(2)]
                    B_eo = [Buf("eo%d" % i) for i in range(2)]
                    ej = sbt(ph, "ej", [128, 128], F32)
                    B_ej = Buf("ej")
                    dfo = [sbt(ph, "dfo%d" % i, [128, 512], BF16) for i in range(4)]
                    B_dfo = [Buf("dfo%d" % i) for i in range(4)]
                    stg = [sbt(ph, "dstg%d" % i, [128, 512], BF16) for i in range(2)]
                    B_stg = [Buf("dstg%d" % i) for i in range(2)]
                    B_dfT = [Buf("dfT%d" % j) for j in range(NB)]
                    ecnt = 0
                    pcnt = 0
                    def c_items(qb_):
                        return [(xo_d[qb_ * 512 + t * 128:qb_ * 512 + (t + 1) * 128, :], 128, t * 128) for t in range(4)]

                    def c_cs_load(qb_):
                        P.dma("sp", lambda e: e.dma_start(out=csq[:, :, :], in_=ropeo_v[:, :, qb_ * 512:(qb_ + 1) * 512]),
                              B_csq, writes=[B_csq])

                    c_cs_load(0)
                    pq = [npipe.prefetch(c_items(0))]
                    npipe.run_group(c_items(0), gmix, G_mix, evac="dve", pre=pq[0])

                    def c_qproj(q):
                        kis = {}
                        for step in range(5):
                            if step < 4:
                                kis[step] = rope_a(wq, Wq, step, 512, 1 + 2 * (step % 2))
                            if step >= 1:
                                hq = step - 1
                                rope_b(kis[hq], 512, csq, B_csq, 1 + 2 * (hq % 2), 2 + 2 * (hq % 2),
                                       lambda hq=hq: dQT[:, hq, :], B_dQT[hq])
                        if q + 1 < NQB:
                            c_cs_load(q + 1)
                            pq[0] = [npipe.load(s_, n_) for (s_, n_, o_) in c_items(q + 1)]

                    pend_out = [None]

                    def c_out(q):
                        pT0 = psT(0)
                        for qs in range(4):
                            sl = qs % 2
                            j = q * 4 + qs

                            def otr(e):
                                last = None
                                for c in range(4):
                                    last = e.transpose(out=pT0[:, c * 128:(c + 1) * 128], in_=dfo[qs][:, c * 128:(c + 1) * 128],
                                                       identity=ident[:, :])
                                return last
                            P.op("pe", otr, reads=[B_dfo[qs], B_ident], writes=[PB[0]])
                            P.op("dve", lambda e: e.tensor_copy(out=stg[sl][:, :], in_=pT0[:, 0:512]), reads=[PB[0]], writes=[B_stg[sl]])
                            P.dma("sp", lambda e: e.dma_start(out=dfT_d[j], in_=stg[sl][:, :]), B_stg[sl],
                                  reads=[B_stg[sl]], writes=[B_dfT[j]])

                    c_qproj(0)
                    for qb in range(NQB):
                        def banks(k):
                            return 1 + 2 * (k % 2)

                        def qk(h, kt, pa):
                            ksz = 128 if kt < NKT - 1 else NMETA

                            def f(e):
                                last = None
                                for c in range(2):
                                    last = e.matmul(ps[0:ksz, pa + c, :], lhsT=dKT[c * 64:(c + 1) * 64, h, kt * 128:kt * 128 + ksz],
                                                    rhs=dQT[c * 64:(c + 1) * 64, h, :], start=True, stop=True)
                                return last
                            P.op("pe", f, reads=[B_dKT[h][min(kt // 4, T // 512)], B_dQT[h]], writes=[PB[pa], PB[pa + 1]])

                        for h in range(4):
                            base = pcnt
                            if h == 0:
                                qk(0, 0, banks(base))
                                qk(0, 1, banks(base + 1))
                            for kt in range(NKT):
                                ksz = 128 if kt < NKT - 1 else NMETA
                                pa = banks(base + kt)
                                sl = (base + kt) % 3
                                P.op("act", lambda e, pa=pa, ksz=ksz, sl=sl: e.activation(
                                    out=PT[sl][0:ksz, :].rearrange("p (b n) -> p b n", b=2), in_=ps[0:ksz, pa:pa + 2, :],
                                    func=AF.Exp, scale=0.125), reads=[PB[pa], PB[pa + 1]], writes=[B_PT[sl]])
                                if h == 0 and kt == 8 and pend_out[0] is not None:
                                    c_out(pend_out[0])
                                    pend_out[0] = None
                                if h == 3 and kt == NKT // 2 and qb + 1 < NQB:
                                    npipe.run_group(c_items(qb + 1), gmix, G_mix, evac="dve", pre=pq[0])
                                if kt + 2 < NKT:
                                    qk(h, kt + 2, banks(base + kt + 2))
                                elif h < 3:
                                    qk(h + 1, kt + 2 - NKT, banks(base + kt + 2))

                                def pv(e, kt=kt, ksz=ksz, sl=sl):
                                    last = None
                                    for c in range(2):
                                        for qs in range(4):
                                            a = c * 4 + qs
                                            last = e.matmul(ps[:, 5 + a // 3, (a % 3) * 129:(a % 3) * 129 + 129],
                                                            lhsT=PT[sl][0:ksz, c * 512 + qs * 128:c * 512 + (qs + 1) * 128],
                                                            rhs=dV[0:ksz, kt, h, :], start=(kt == 0 and a % 3 == 0),
                                                            stop=(kt == NKT - 1), skip_group_check=True)
                                    return last
                                P.op("pe", pv, reads=[B_PT[sl], B_dV[kt]], writes=[PB[5], PB[6], PB[7]])
                            pcnt = base + NKT
                            P.op("dve", lambda e: e.tensor_copy(out=ocp[:, :, :], in_=ps[:, 5:8, 0:387]),
                                 reads=[PB[5], PB[6], PB[7]], writes=[B_ocp])
                            if h == 0 and qb + 1 < NQB:
                                for hnd in pq[0]:
                                    npipe.stats(hnd)
                            if h == 3 and qb + 1 < NQB:
                                c_qproj(qb + 1)
                            for qs in range(4):
                                i = ecnt % 2
                                ecnt += 1
                                a0, a1 = qs, 4 + qs
                                O0 = ocp[:, a0 // 3, (a0 % 3) * 129:(a0 % 3) * 129 + 129]
                                O1 = ocp[:, a1 // 3, (a1 % 3) * 129:(a1 % 3) * 129 + 129]
                                E, BE = est[i], B_est[i]
                                P.op("dve", lambda e, E=E, O0=O0: e.reciprocal(out=E[:, 0:1], in_=O0[:, 128:129]),
                                     reads=[B_ocp], writes=[BE])
                                P.op("dve", lambda e, E=E, O1=O1: e.reciprocal(out=E[:, 1:2], in_=O1[:, 128:129]),
                                     reads=[B_ocp], writes=[BE])
                                P.op("dve", lambda e, E=E: e.tensor_tensor(out=E[:, 2:3], in0=E[:, 1:2], in1=lst[:, 5:6], op=ALU.mult),
                                     reads=[BE, B_lst], writes=[BE])
                                P.op("dve", lambda e, E=E, O1=O1, i=i: e.tensor_scalar(out=etmp[i][:, :], in0=O1[:, 0:128], scalar1=E[:, 2:3],
                                                                                  scalar2=None, op0=ALU.mult),
                                     reads=[B_ocp, BE], writes=[B_etmp[i]])
                                P.op("dve", lambda e, E=E, O0=O0, i=i: e.scalar_tensor_tensor(
                                    out=eo[i][:, :], in0=O0[:, 0:128], scalar=E[:, 0:1], in1=etmp[i][:, :], op0=ALU.mult, op1=ALU.add),
                                    reads=[B_ocp, BE, B_etmp[i]], writes=[B_eo[i]])
                                P.op("dve", lambda e, E=E, i=i: e.scalar_tensor_tensor(
                                    out=ej[:, :], in0=eo[i][:, :], scalar=1.0, in1=eo[i][:, :], op0=ALU.mult, op1=ALU.mult,
                                    accum_out=E[:, 3:4]), reads=[B_eo[i]], writes=[B_ej, BE])
                                P.op("pool", lambda e, E=E: e.tensor_scalar(out=E[:, 4:5], in0=E[:, 3:4], scalar1=1.0 / 128, scalar2=1e-5,
                                                                            op0=ALU.mult, op1=ALU.add), reads=[BE], writes=[BE])
                                P.op("pool", lambda e, E=E: e.tensor_tensor(out=E[:, 5:6], in0=E[:, 4:5], in1=mhalf[:, 0:1], op=ALU.pow),
                                     reads=[BE, B_mhalf], writes=[BE])
                                P.op("dve", lambda e, E=E, i=i, qs=qs: e.scalar_tensor_tensor(
                                    out=dfo[qs][:, h * 128:(h + 1) * 128], in0=eo[i][:, :], scalar=E[:, 5:6], in1=sgn[:, :],
                                    op0=ALU.mult, op1=ALU.mult), reads=[B_eo[i], BE, B_sgn], writes=[B_dfo[qs]])
                        if qb + 1 < NQB:
                            pend_out[0] = qb
                        else:
                            c_out(qb)
                    P.barrier()
                    if STOP_AFTER <= 4:
                        return

            B_x2 = [Buf("x2_%d" % j) for j in range(NB)]
            ffb = [0, 2 * 128, 6 * 128, 11 * 128, 16 * 128, FC * 128]
            wg_v = w_gate_d.rearrange("(c p) n -> p c n", p=128)
            wu_v = w_up_d.rearrange("(c p) n -> p c n", p=128)
            dd = st.enter_context(ExitStack())
            wgt = sbt(dd, "wgt", [128, KC, D_FF], BF16)
            Wgt_p = [Buf("wgt_p%d" % pi) for pi in range(len(ffb) - 1)]
            with ExitStack() as ph:
                gmix, G_mix = load_gain(ph, "gmixD", gmix_d)
                wg, Wg2 = load_w_pieces(ph, "wg", w_in_v, 3072, [0, 512, 1024, 1536, 2048])
                wno, Wno = load_w(ph, "wno", w_no_d.rearrange("(c p) n -> p c n", p=128), 0, D, nch=4)
                wdo, Wdo = load_w(ph, "wdo", w_do_d.rearrange("(c p) n -> p c n", p=128), 0, D, nch=4)
                wo, Wo = load_w(ph, "wo", w_o_d.rearrange("(c p) n -> p c n", p=128), 0, D)
                for pi in range(len(ffb) - 1):
                    P.dma("pool", lambda e, lo=ffb[pi], hi=ffb[pi + 1]: e.dma_start(out=wgt[:, :, lo:hi], in_=wg_v[:, :, lo:hi]),
                          Wgt_p[pi], writes=[Wgt_p[pi]])
                npipe = NormPipe(ph, "d1", 4, [0])
                XT = [Buf("xnTa"), Buf("xnTb")]
                naTt = [sbt(ph, "naTt%d" % i, [128, 512], BF16) for i in range(3)]
                dfTt = [sbt(ph, "dfTt%d" % i, [128, 512], BF16) for i in range(3)]
                B_naTt = [Buf("naTt%d" % i) for i in range(3)]
                B_dfTt = [Buf("dfTt%d" % i) for i in range(3)]
                sgt = sbt(ph, "sgt", [128, 2048], BF16)
                B_sgn_, B_sgd_ = Buf("sgt_na"), Buf("sgt_df")
                m1 = sbt(ph, "m1", [128, D], F32)
                m2 = sbt(ph, "m2", [128, D], F32)
                mb = sbt(ph, "mb", [128, D], BF16)
                mT = sbt(ph, "mT", [128, KC, 128], BF16)
                B_m1, B_m2, B_mb, B_mT = Buf("m1"), Buf("m2"), Buf("mb"), Buf("mT")
                x2s = [sbt(ph, "x2st%d" % i, [128, D], F32) for i in range(2)]
                B_x2s = [Buf("x2st%d" % i) for i in range(2)]
                hs = {}

                def d1_loads(j):
                    hs[j] = npipe.load(xo_d[j * 128:(j + 1) * 128, :], 128)
                    sl = j % 3
                    P.dma("sp", lambda e: e.dma_start(out=naTt[sl][:, :], in_=naT_d[j]), B_naTt[sl],
                          reads=[B_naT[j]], writes=[B_naTt[sl]])
                    P.dma("sp", lambda e: e.dma_start(out=dfTt[sl][:, :], in_=dfT_d[j]), B_dfTt[sl],
                          reads=[B_dfT[j]], writes=[B_dfTt[sl]])

                def d1_gate(j, which):
                    off = (j % 2) * 128
                    b0 = 1 + 2 * which
                    for half in range(2):
                        P.op("pe", mm_group(ps[:, b0 + half, :], lambda c: xnT[:, c, off:off + 128],
                                            lambda c: wg[:, c, which * 1024 + half * 512:which * 1024 + (half + 1) * 512], KC),
                             reads=[Wg2[which * 2 + half], XT[j % 2]], writes=[PB[b0 + half]])
                    P.op("act", lambda e: e.activation(out=sgt[:, which * 1024:(which + 1) * 1024].rearrange("p (b n) -> p b n", b=2),
                                                       in_=ps[:, b0:b0 + 2, :], func=AF.Sigmoid),
                         reads=[PB[b0], PB[b0 + 1]], writes=[B_sgd_ if which else B_sgn_])

                def d1_branch(j, which):
                    sl = j % 3
                    src_t, SRC, w, W = (dfTt[sl], B_dfTt[sl], wdo, Wdo) if which else (naTt[sl], B_naTt[sl], wno, Wno)
                    for half in range(2):
                        P.op("pe", mm_group(ps[:, 5 + half, :], lambda c: src_t[:, c * 128:(c + 1) * 128],
                                            lambda c: w[:, c, half * 512:(half + 1) * 512], 4),
                             reads=[W, SRC], writes=[PB[5 + half]])
                    dst, DST = (m2, B_m2) if which else (m1, B_m1)
                    P.op("dve", lambda e: e.tensor_tensor(out=dst[:, :].rearrange("p (b n) -> p b n", b=2), in0=ps[:, 5:7, :],
                                                          in1=sgt[:, which * 1024:(which + 1) * 1024].rearrange("p (b n) -> p b n", b=2),
                                                          op=ALU.mult),
                         reads=[PB[5], PB[6], B_sgd_ if which else B_sgn_], writes=[DST])
                    if which:
                        P.op("pool", lambda e: e.tensor_tensor(out=mb[:, :], in0=m1[:, :], in1=m2[:, :], op=ALU.add),
                             reads=[B_m1, B_m2], writes=[B_mb])

                def d1_mT(j):
                    pT7 = psT(7)

                    def mtr(e):
                        last = None
                        for c in range(KC):
                            last = e.transpose(out=pT7[:, c * 128:(c + 1) * 128], in_=mb[:, c * 128:(c + 1) * 128], identity=ident[:, :])
                        return last
                    P.op("pe", mtr, reads=[B_mb, B_ident], writes=[PB[7]])
                    P.op("act", lambda e: e.copy(out=mT[:, :, :], in_=pT7.rearrange("p (c t) -> p c t", c=KC)), reads=[PB[7]], writes=[B_mT])

                def d1_x2(j):
                    sl = j % 2
                    xs, XS = hs[j]["xs"], hs[j]["XS"]
                    for half in range(2):
                        P.op("pe", mm_group(ps[:, 1 + half, :], lambda c: mT[:, c, :],
                                            lambda c: wo[:, c, half * 512:(half + 1) * 512], KC),
                             reads=[Wo, B_mT], writes=[PB[1 + half]])
                    P.op("dve", lambda e: e.tensor_tensor(out=x2s[sl][:, :].rearrange("p (b n) -> p b n", b=2), in0=ps[:, 1:3, :],
                                                          in1=xs[:, :].rearrange("p (b n) -> p b n", b=2), op=ALU.add),
                         reads=[PB[1], PB[2], XS], writes=[B_x2s[sl]])
                    P.dma("sp", lambda e: e.dma_start(out=x2_d[j * 128:(j + 1) * 128, :], in_=x2s[sl][:, :]), B_x2s[sl],
                          reads=[B_x2s[sl]], writes=[B_x2[j]])

                d1_loads(0)
                d1_loads(1)
                npipe.norm(hs[0], gmix, G_mix)
                npipe.transpose(hs[0], 0, XT[0], "act")
                for j in range(NB):
                    if j + 2 < NB:
                        d1_loads(j + 2)
                    d1_gate(j, 0)
                    if j + 1 < NB:
                        npipe.norm(hs[j + 1], gmix, G_mix)
                    if j >= 1:
                        d1_mT(j - 1)
                    d1_branch(j, 0)
                    d1_gate(j, 1)
                    if j >= 1:
                        d1_x2(j - 1)
                    if j + 1 < NB:
                        npipe.transpose(hs[j + 1], ((j + 1) % 2) * 128, XT[(j + 1) % 2], "act")
                    d1_branch(j, 1)
                d1_mT(NB - 1)
                d1_x2(NB - 1)
                P.barrier()
                if STOP_AFTER <= 5:
                    return

            with ExitStack() as ph:
                gffn, G_ffn = load_gain(ph, "gffn", gffn_d)
                gfin, G_fin = load_gain(ph, "gfin", gfin_d)
                wup = sbt(ph, "wup", [128, KC, D_FF], BF16)
                Wup_p = [Buf("wup_p%d" % pi) for pi in range(len(ffb) - 1)]

                def ffpiece(c):
                    for pi in range(len(ffb) - 1):
                        if c * 128 < ffb[pi + 1]:
                            return pi
                npipe = NormPipe(ph, "d2", 4, [0])
                XT = [Buf("xnTa2"), Buf("xnTb2")]
                G2 = 256
                NG = NOWN // G2
                actT = sbt(ph, "actT", [128, FC, G2], BF16)
                B_actT = [Buf("actT%d" % c) for c in range(FC)]
                sgl = [sbt(ph, "sgl%d" % i, [128, G2], F32) for i in range(2)]
                B_sgl = [Buf("sgl%d" % i) for i in range(2)]
                x3 = sbt(ph, "x3", [128, D], F32)
                B_x3 = Buf("x3")
                fst = sbt(ph, "fst", [128, 4], F32)
                B_fst = Buf("fst")
                ost = [sbt(ph, "ost%d" % i, [128, D], F32) for i in range(2)]
                B_ost = [Buf("ost%d" % i) for i in range(2)]
                hs = {}
                cnts = {"o": 0, "c": 0}

                def d2_loads(g):
                    for t in range(G2 // 128):
                        j = g * (G2 // 128) + t
                        hs[g, t] = npipe.load(x2_d[j * 128:(j + 1) * 128, :], 128)

                def d2_norm(g):
                    for t in range(G2 // 128):
                        npipe.norm(hs[g, t], gffn, G_ffn)

                def d2_T(g):
                    for t in range(G2 // 128):
                        npipe.transpose(hs[g, t], (g % 2) * G2 + t * 128, XT[g % 2], "act")

                def d2_normT(g):
                    d2_norm(g)
                    d2_T(g)

                def d2_gu(g, c):
                    off = (g % 2) * G2
                    ccnt = cnts["c"]
                    cnts["c"] += 1
                    pg = 1 + 2 * (ccnt % 2)
                    pu = pg + 1
                    i = ccnt % 2
                    P.op("pe", mm_group(ps[:, pg, 0:G2], lambda k: wgt[:, k, c * 128:(c + 1) * 128],
                                        lambda k: xnT[:, k, off:off + G2], KC), reads=[Wgt_p[ffpiece(c)], XT[g % 2]], writes=[PB[pg]])
                    P.op("pe", mm_group(ps[:, pu, 0:G2], lambda k: wup[:, k, c * 128:(c + 1) * 128],
                                        lambda k: xnT[:, k, off:off + G2], KC), reads=[Wup_p[ffpiece(c)], XT[g % 2]], writes=[PB[pu]])
                    P.op("act", lambda e: e.activation(out=sgl[i][:, :], in_=ps[:, pg, 0:G2], func=AF.Silu),
                         reads=[PB[pg]], writes=[B_sgl[i]])
                    P.op("dve", lambda e: e.tensor_tensor(out=actT[:, c, :], in0=ps[:, pu, 0:G2], in1=sgl[i][:, :], op=ALU.mult),
                         reads=[PB[pu], B_sgl[i]], writes=[B_actT[c]])

                def d2_out(g):
                    for t in range(G2 // 128):
                        j = g * (G2 // 128) + t
                        xs, XS = hs[g, t]["xs"], hs[g, t]["XS"]
                        ybank = (5, 6) if t % 2 == 0 else (7, 0)
                        for half in range(2):
                            yb = ybank[half]
                            for lo in (0, FC // 2):
                                P.op("pe", mm_group(ps[:, yb, :], lambda c: actT[:, c, t * 128:(t + 1) * 128],
                                                    lambda c: wdn[:, c, half * 512:(half + 1) * 512], FC, lo, lo + FC // 2),
                                     reads=[Wdn] + B_actT, writes=[PB[yb]])
                        for half in range(2):
                            yb = ybank[half]
                            P.op("dve", lambda e: e.tensor_tensor(out=x3[:, half * 512:(half + 1) * 512], in0=ps[:, yb, :],
                                                                  in1=xs[:, half * 512:(half + 1) * 512], op=ALU.add),
                                 reads=[PB[yb], XS], writes=[B_x3])
                        rms_rstd(x3, B_x3, 128, fst, B_fst, D, 1e-6)
                        oi = cnts["o"] % 2
                        cnts["o"] += 1
                        P.op("dve", lambda e: e.scalar_tensor_tensor(out=ost[oi][:, :], in0=x3[:, :], scalar=fst[:, 2:3], in1=gfin[:, :],
                                                                     op0=ALU.mult, op1=ALU.mult),
                             reads=[B_x3, B_fst, G_fin], writes=[B_ost[oi]])
                        P.dma("sp", lambda e: e.dma_start(out=out_d[j * 128:(j + 1) * 128, :], in_=ost[oi][:, :]), B_ost[oi],
                              reads=[B_ost[oi]])

                d2_loads(0)
                first_x = [hs[0, t]["XS"] for t in range(G2 // 128)] + [G_ffn]
                for pi in range(len(ffb) - 1):
                    P.dma("pool", lambda e, lo=ffb[pi], hi=ffb[pi + 1]: e.dma_start(out=wup[:, :, lo:hi], in_=wu_v[:, :, lo:hi]),
                          Wup_p[pi], reads=first_x if pi == 0 else (), writes=[Wup_p[pi]])
                wdn, Wdn = load_w(ph, "wdn", w_down_d.rearrange("(c p) n -> p c n", p=128), 0, D, nch=FC)
                d2_normT(0)
                for g in range(NG):
                    if g + 1 < NG:
                        d2_loads(g + 1)
                    for c in range(FC):
                        d2_gu(g, c)
                        if c == 3 and g + 1 < NG:
                            d2_norm(g + 1)
                        if c == FC // 2 + 2 and g + 1 < NG:
                            d2_T(g + 1)
                    d2_out(g)
                P.barrier()
                if STOP_AFTER <= 6:
                    return
        body()
        P.check_deadlock()
    return nc


def _rope_tables(pos):
    half = 32
    inv = (np.float32(10000.0) ** (-np.arange(half, dtype=np.float32) / np.float32(half))).astype(np.float32)
    ang = pos.astype(np.float32)[None, :] * inv[:, None]
    cos = np.cos(ang).astype(np.float32)
    sin = np.sin(ang).astype(np.float32)
    tab = np.empty((2, 128, pos.shape[0]), np.float32)
    for p in range(128):
        d = p % 64
        f = d % 32
        tab[0, p] = cos[f]
        tab[1, p] = -sin[f] if d < 32 else sin[f]
    return tab


def _na_bias_tables(rpb, s, ROWS):
    R_OWN = ROWS // 2
    NB = R_OWN // 2
    g0 = s * R_OWN - 4
    kr = min(8, ROWS)
    kc = np.arange(GRID_W)
    qc = np.arange(GRID_W)
    cs = np.clip(qc - 8, 0, GRID_W - 16)
    colvalid = (kc[:, None] >= cs[None, :]) & (kc[:, None] < cs[None, :] + 16)
    colidx = np.clip(kc[:, None] - qc[None, :] + 15, 0, 30)

    def tile_table(j, kt):
        tab = np.full((128, 8, 128), NEG, np.float32)
        for ko in range(2):
            krow = g0 + 2 * kt + ko
            if krow < 0 or krow >= ROWS:
                continue
            for qo in range(2):
                r = s * R_OWN + 2 * j + qo
                rs = min(max(r - kr // 2, 0), ROWS - kr)
                if not (rs <= krow < rs + kr):
                    continue
                ridx = krow - r + 7
                vals = rpb[:, ridx, :][:, colidx]
                blk = np.where(colvalid[None], vals, np.float32(NEG)).astype(np.float32)
                tab[ko * 64:(ko + 1) * 64, :, qo * 64:(qo + 1) * 64] = blk.transpose(1, 0, 2)
        return np.ascontiguousarray(tab[:, [0, 2, 4, 6, 1, 3, 5, 7], :]).reshape(128, 1024)

    jint = 2 if NB > 4 else 0
    spec = ([(jint, jint + i) for i in range(5)] + [(0, i) for i in range(6)] + [(1, 1 + i) for i in range(5)]
            + [(NB - 2, NB - 2 + i) for i in range(5)] + [(NB - 1, NB - 2 + i) for i in range(6)])
    if s == 1 or NB <= 4:
        pass
    return np.stack([tile_table(j, kt) for (j, kt) in spec], 0)


def make_in_maps(inputs, ROWS, nbatch):
    T = ROWS * GRID_W
    R_OWN = ROWS // 2
    NOWN = R_OWN * GRID_W
    WT = (R_OWN + 8) // 2
    f = lambda a: np.ascontiguousarray(np.asarray(a, dtype=np.float32))
    x = f(inputs["x"])
    meta = f(inputs["meta_tokens"])
    w_in = f(inputs["w_in"][0])
    shared = {
        "w_in": w_in,
        "w_na_out": f(inputs["w_na_out"][0]), "w_diff_out": f(inputs["w_diff_out"][0]), "w_o": f(inputs["w_o"][0]),
        "w_gate": f(inputs["w_gate"][0]), "w_up": f(inputs["w_up"][0]), "w_down": f(inputs["w_down"][0]),
        "mix_norm": f(inputs["mix_norm"][0:1]), "ffn_norm": f(inputs["ffn_norm"][0:1]),
        "final_norm": f(inputs["final_norm"]).reshape(1, D), "diff_subln": f(inputs["diff_subln"][0:1]),
        "lamv": np.concatenate([f(inputs["lambda_q1"][0]), f(inputs["lambda_k1"][0]),
                                f(inputs["lambda_q2"][0]), f(inputs["lambda_k2"][0])]).reshape(1, 256),
        "ident": np.eye(128, dtype=np.float32),
    }
    psw = np.zeros((128, 128), np.float32)
    for m in range(128):
        d = m % 64
        psw[(m - d) + ((d + 32) % 64), m] = 1.0
    shared["pswap"] = psw
    posb = np.concatenate([NMETA + np.arange(T), np.arange(NMETA)])
    ropeb = _rope_tables(posb)
    rpb = f(inputs["na_rpb"][0])
    bias_s = [_na_bias_tables(rpb, s, ROWS) for s in range(2)]
    maps = []
    for b in range(nbatch):
        xbat = np.concatenate([x[b], meta], 0)
        for s in range(2):
            own0 = s * NOWN
            xw = np.zeros((WT * 128 + NMETA, D), np.float32)
            g0 = s * R_OWN - 4
            for lr in range(R_OWN + 8):
                gr = g0 + lr
                if 0 <= gr < ROWS:
                    xw[lr * 64:(lr + 1) * 64] = x[b, gr * 64:(gr + 1) * 64]
            xw[WT * 128:] = meta
            m = dict(shared)
            m["xb"] = xbat
            m["xw"] = xw
            m["xo"] = np.ascontiguousarray(x[b, own0:own0 + NOWN])
            m["ropeb"] = ropeb
            m["ropeo"] = np.ascontiguousarray(ropeb[:, :, own0:own0 + NOWN])
            m["na_bias"] = bias_s[s]
            maps.append(m)
    return maps


_NC_CACHE = {}


def run(inputs, ROWS, nbatch):
    if ROWS not in _NC_CACHE:
        _NC_CACHE[ROWS] = build(ROWS)
    nc = _NC_CACHE[ROWS]
    maps = make_in_maps(inputs, ROWS, nbatch)
    res = run_bass_kernel_spmd(nc, maps, core_ids=list(range(len(maps))))
    T = ROWS * GRID_W
    out = np.empty((nbatch, T, D), np.float32)
    NOWN = T // 2
    for b in range(nbatch):
        for s in range(2):
            out[b, s * NOWN:(s + 1) * NOWN] = res.results[b * 2 + s]["out"]
    return out


def kernel(**inputs):
    return run(inputs, 128, 4)
```
